# Optimizing a Trainium2 kernel written in Bass

```python
import math
import jax
import jax.numpy as jnp
from jax import lax
import numpy as np

D_MODEL = 2048
BATCH = 1
SEQ = 16384
DEPTH = 4

GRID_W = 64
CTX_LEN = 256
N_MIXERS = 3
N_HYENA_LAYERS = (DEPTH + 2) // 3
N_ATTN_LAYERS = (DEPTH + 1) // 3
N_POOL_LAYERS = DEPTH // 3
N_MOD = 9
NORM_EPS = 1e-6
D_FF = 5632
FFN_RES = 0.5
FILTER_BANDS = 16
FILTER_EMB = 1 + 2 * FILTER_BANDS
FILTER_HIDDEN = 64
DECAY_TARGET = 1e-2
SHORT_DECAY_PCT = 0.3
LONG_DECAY_PCT = 1.5
HEAD_DIM = 128
N_HEADS = D_MODEL // HEAD_DIM
N_KV_HEADS = 4
GROUP = N_HEADS // N_KV_HEADS
WINDOW = 128
ATTN_BLOCK = 128
ROPE_THETA = 10000.0
ROPE_PAIRS = HEAD_DIM // 4
QKV_DIM = (N_HEADS + 2 * N_KV_HEADS) * HEAD_DIM
POOL_SIZES = (2, 4, 8, 16)
POOL_GROUP = D_MODEL // len(POOL_SIZES)

kernel_name = 'hybrid_hyena_swa_pool_diffusion_trunk'


def rms_norm(x, g):
    xf = x.astype(jnp.float32)
    y = xf * lax.rsqrt(jnp.mean(xf * xf, axis=-1, keepdims=True) + NORM_EPS)
    return (y * g.astype(jnp.float32)).astype(x.dtype)


def modulate(h, g, mod, k):
    return rms_norm(h, g) * (1.0 + mod[:, 3 * k + 1]) + mod[:, 3 * k]


def gated_residual(h, y, g, mod, k, weight):
    return h + weight * mod[:, 3 * k + 2] * rms_norm(y, g)


def swiglu(u, w_in, w_out):
    gate, up = jnp.split(u @ w_in, 2, axis=-1)
    return (jax.nn.silu(gate) * up) @ w_out


def short_conv3(u, w, b):
    up = jnp.pad(u, ((0, 0), (1, 1), (0, 0)))
    return up[:, :-2] * w[0] + u * w[1] + up[:, 2:] * w[2] + b


def implicit_filter(L, w1, b1, w2, b2, w3, b3, w4, freq):
    D = w4.shape[1] // 2
    t = jnp.linspace(0.0, 1.0, L, dtype=jnp.float32)[:, None]
    omega = 2.0 * math.pi * jnp.arange(L, dtype=jnp.float32)[:, None] / L
    bands = jnp.linspace(1e-4, FILTER_BANDS - 1, FILTER_BANDS, dtype=jnp.float32)[None, :]
    z = jnp.concatenate([t, jnp.cos(bands * omega), -jnp.sin(bands * omega)], axis=-1)
    f = jnp.sin(freq[0] * (z @ w1 + b1))
    f = jnp.sin(freq[1] * (f @ w2 + b2))
    f = jnp.sin(freq[2] * (f @ w3 + b3))
    f = (f @ w4).astype(jnp.float32)
    deltas = jnp.abs(jnp.linspace(math.log(DECAY_TARGET) / LONG_DECAY_PCT,
                                  math.log(DECAY_TARGET) / SHORT_DECAY_PCT, D, dtype=jnp.float32))
    decay = jnp.exp(-t * deltas[None, :])
    h_fwd = f[:, :D] * decay
    h_bwd = f[:, D:] * decay
    return jnp.concatenate([h_fwd, jnp.zeros((1, D), jnp.float32), h_bwd[1:][::-1]], axis=0)


def long_conv(u, filt, skip):
    L = u.shape[1]
    uf = jnp.fft.rfft(u.astype(jnp.float32), n=2 * L, axis=1)
    ff = jnp.fft.rfft(filt, n=2 * L, axis=0)
    y = jnp.fft.irfft(uf * ff[None], n=2 * L, axis=1)[:, :L]
    return (y + u.astype(jnp.float32) * skip.astype(jnp.float32)).astype(u.dtype)


def hyena_mixer(u, w_in, b_in, w_sc, b_sc, f_w1, f_b1, f_w2, f_b2, f_w3, f_b3, f_w4, f_freq, skip, w_out, b_out):
    L = u.shape[1]
    z = short_conv3(u @ w_in + b_in, w_sc, b_sc)
    x0, x1, v = jnp.split(z, 3, axis=-1)
    filt = implicit_filter(L, f_w1, f_b1, f_w2, f_b2, f_w3, f_b3, f_w4, f_freq)
    y = x0 * long_conv(v * x1, filt, skip)
    return y @ w_out + b_out


def axial_rope(x, ang_row, ang_col):
    extra = x.ndim - 3

    def rot(v, ang):
        ang = ang.reshape((ang.shape[0],) + (1,) * extra + (ang.shape[1],))
        cos, sin = jnp.cos(ang).astype(v.dtype), jnp.sin(ang).astype(v.dtype)
        v1, v2 = jnp.split(v, 2, axis=-1)
        return jnp.concatenate([v1 * cos - v2 * sin, v2 * cos + v1 * sin], axis=-1)

    xr, xc = jnp.split(x, 2, axis=-1)
    return jnp.concatenate([rot(xr, ang_row), rot(xc, ang_col)], axis=-1)


def windowed_gqa(u, uc, w_qkv, b_qkv, sink, w_o, b_o, ang_row, ang_col, ctx_queries):
    B, L, D = u.shape
    C = uc.shape[1]
    nb = L // ATTN_BLOCK
    scale = HEAD_DIM ** -0.5
    qd, kd = N_HEADS * HEAD_DIM, N_KV_HEADS * HEAD_DIM
    qkv = u @ w_qkv + b_qkv
    q = qkv[..., :qd].reshape(B, L, N_KV_HEADS, GROUP, HEAD_DIM)
    k = qkv[..., qd:qd + kd].reshape(B, L, N_KV_HEADS, HEAD_DIM)
    v = qkv[..., qd + kd:].reshape(B, L, N_KV_HEADS, HEAD_DIM)
    q = axial_rope(q, ang_row, ang_col)
    k = axial_rope(k, ang_row, ang_col)
    qkv_c = uc @ w_qkv + b_qkv
    kc = qkv_c[..., qd:qd + kd].reshape(B, C, N_KV_HEADS, HEAD_DIM)
    vc = qkv_c[..., qd + kd:].reshape(B, C, N_KV_HEADS, HEAD_DIM)
    sink_l = sink.astype(jnp.float32).reshape(N_KV_HEADS, GROUP)[None, :, :, None, None, None]

    qb = q.reshape(B, nb, ATTN_BLOCK, N_KV_HEADS, GROUP, HEAD_DIM)

    def neighbours(t):
        tp = jnp.pad(t, ((0, 0), (ATTN_BLOCK, ATTN_BLOCK), (0, 0), (0, 0)))
        tp = tp.reshape(B, nb + 2, ATTN_BLOCK, N_KV_HEADS, HEAD_DIM)
        return jnp.concatenate([tp[:, :-2], tp[:, 1:-1], tp[:, 2:]], axis=2)

    kb, vb = neighbours(k), neighbours(v)
    qi = jnp.arange(ATTN_BLOCK)[:, None]
    si = jnp.arange(3 * ATTN_BLOCK)[None, :]
    rel = si - ATTN_BLOCK - qi
    key_pos = (jnp.arange(nb)[:, None, None] - 1) * ATTN_BLOCK + si[None]
    valid = (jnp.abs(rel) <= WINDOW)[None] & (key_pos >= 0) & (key_pos < L)

    s_loc = jnp.einsum('bnqkgd,bnskd->bkgnqs', qb, kb, preferred_element_type=jnp.float32) * scale
    s_loc = jnp.where(valid, s_loc, -jnp.inf)
    s_ctx = jnp.einsum('bnqkgd,bckd->bkgnqc', qb, kc, preferred_element_type=jnp.float32) * scale
    mx = jnp.maximum(jnp.maximum(s_loc.max(-1, keepdims=True), s_ctx.max(-1, keepdims=True)), sink_l)
    p_loc = jnp.exp(s_loc - mx)
    p_ctx = jnp.exp(s_ctx - mx)
    denom = p_loc.sum(-1, keepdims=True) + p_ctx.sum(-1, keepdims=True) + jnp.exp(sink_l - mx)
    o = (jnp.einsum('bkgnqs,bnskd->bnqkgd', (p_loc / denom).astype(vb.dtype), vb)
         + jnp.einsum('bkgnqc,bckd->bnqkgd', (p_ctx / denom).astype(vc.dtype), vc))
    y = o.reshape(B, L, D) @ w_o + b_o

    yc = None
    if ctx_queries:
        qc = qkv_c[..., :qd].reshape(B, C, N_KV_HEADS, GROUP, HEAD_DIM)
        sc = jnp.einsum('bckgd,bekd->bkgce', qc, kc, preferred_element_type=jnp.float32) * scale
        sink_c = sink_l[..., 0, :, :]
        mxc = jnp.maximum(sc.max(-1, keepdims=True), sink_c)
        pc = jnp.exp(sc - mxc)
        den_c = pc.sum(-1, keepdims=True) + jnp.exp(sink_c - mxc)
        oc = jnp.einsum('bkgce,bekd->bckgd', (pc / den_c).astype(vc.dtype), vc)
        yc = oc.reshape(B, C, D) @ w_o + b_o
    return y, yc


def pool_mixer(u, w, b, scale):
    B, L, D = u.shape
    uf = u.astype(jnp.float32)
    csum = jnp.concatenate([jnp.zeros((B, 1, D), jnp.float32), jnp.cumsum(uf, axis=1)], axis=1)
    t = jnp.arange(L)
    parts = []
    for g, size in enumerate(POOL_SIZES):
        lo = jnp.clip(t - size // 2, 0, L)
        hi = jnp.clip(t - size // 2 + size, 0, L)
        cs = csum[..., g * POOL_GROUP:(g + 1) * POOL_GROUP]
        mean = (cs[:, hi] - cs[:, lo]) / (hi - lo).astype(jnp.float32)[:, None]
        parts.append(mean - uf[..., g * POOL_GROUP:(g + 1) * POOL_GROUP])
    y = jnp.stack(parts, axis=2).astype(u.dtype)
    y = jnp.einsum('blgc,gcd->blgd', y, w).reshape(B, L, D) + b
    return y * scale


def setup_inputs(seed: int = 0) -> dict:
    key = jax.random.key(seed)
    keys = iter(jax.random.split(key, 48))
    D = D_MODEL

    def nrm(shape, std):
        return jax.random.normal(next(keys), shape, jnp.float32) * std

    NH, NA, NP = N_HYENA_LAYERS, N_ATTN_LAYERS, N_POOL_LAYERS
    return {
        'x': nrm((BATCH, SEQ, D), 1.0),
        'c': nrm((BATCH, D), 1.0),
        'ctx': nrm((BATCH, CTX_LEN, D), 1.0),
        'c_ctx': nrm((D,), 1.0),
        'w_ada': nrm((DEPTH, D, N_MOD * D), 0.5 * D ** -0.5),
        'b_ada': nrm((DEPTH, N_MOD * D), 0.02),
        'norm_pre': 1.0 + nrm((DEPTH, 3, D), 0.05),
        'norm_post': 1.0 + nrm((DEPTH, 3, D), 0.05),
        'w_ffn_in': nrm((DEPTH, 2, D, 2 * D_FF), D ** -0.5),
        'w_ffn_out': nrm((DEPTH, 2, D_FF, D), D_FF ** -0.5),
        'hy_w_in': nrm((NH, D, 3 * D), D ** -0.5),
        'hy_b_in': nrm((NH, 3 * D), 0.02),
        'hy_w_sc': nrm((NH, 3, 3 * D), 0.5),
        'hy_b_sc': nrm((NH, 3 * D), 0.02),
        'hy_f_w1': nrm((NH, FILTER_EMB, FILTER_HIDDEN), FILTER_EMB ** -0.5),
        'hy_f_b1': nrm((NH, FILTER_HIDDEN), 0.1),
        'hy_f_w2': nrm((NH, FILTER_HIDDEN, FILTER_HIDDEN), FILTER_HIDDEN ** -0.5),
        'hy_f_b2': nrm((NH, FILTER_HIDDEN), 0.1),
        'hy_f_w3': nrm((NH, FILTER_HIDDEN, FILTER_HIDDEN), FILTER_HIDDEN ** -0.5),
        'hy_f_b3': nrm((NH, FILTER_HIDDEN), 0.1),
        'hy_f_w4': nrm((NH, FILTER_HIDDEN, 2 * D), FILTER_HIDDEN ** -0.5),
        'hy_f_freq': 1.0 + nrm((NH, 3, FILTER_HIDDEN), 0.1),
        'hy_skip': nrm((NH, D), 1.0),
        'hy_w_out': nrm((NH, D, D), D ** -0.5),
        'hy_b_out': nrm((NH, D), 0.02),
        'at_w_qkv': nrm((NA, D, QKV_DIM), D ** -0.5),
        'at_b_qkv': nrm((NA, QKV_DIM), 0.02),
        'at_sink': nrm((NA, N_HEADS), 0.5),
        'at_w_o': nrm((NA, D, D), D ** -0.5),
        'at_b_o': nrm((NA, D), 0.02),
        'pl_w': nrm((NP, len(POOL_SIZES), POOL_GROUP, POOL_GROUP), POOL_GROUP ** -0.5),
        'pl_b': nrm((NP, D), 0.02),
        'pl_scale': 1.0 + nrm((NP, D), 0.1),
    }


def reference(x, c, ctx, c_ctx, w_ada, b_ada, norm_pre, norm_post, w_ffn_in, w_ffn_out,
              hy_w_in, hy_b_in, hy_w_sc, hy_b_sc, hy_f_w1, hy_f_b1, hy_f_w2, hy_f_b2,
              hy_f_w3, hy_f_b3, hy_f_w4, hy_f_freq, hy_skip, hy_w_out, hy_b_out,
              at_w_qkv, at_b_qkv, at_sink, at_w_o, at_b_o,
              pl_w, pl_b, pl_scale):
    B, L, D = x.shape
    ROWS = L // GRID_W
    pos_row = jnp.broadcast_to(jnp.arange(ROWS)[:, None], (ROWS, GRID_W)).reshape(-1).astype(jnp.float32)
    pos_col = jnp.broadcast_to(jnp.arange(GRID_W)[None, :], (ROWS, GRID_W)).reshape(-1).astype(jnp.float32)
    inv_freq = ROPE_THETA ** (-jnp.arange(ROPE_PAIRS, dtype=jnp.float32) / ROPE_PAIRS)
    ang_row = pos_row[:, None] * inv_freq[None, :]
    ang_col = pos_col[:, None] * inv_freq[None, :]

    attn_layers = [i for i in range(DEPTH) if i % N_MIXERS == 1]
    last_ctx_layer = attn_layers[-1] if attn_layers else -1

    h, hc = x, ctx
    for i in range(DEPTH):
        kind, j = i % N_MIXERS, i // N_MIXERS
        ctx_live = i <= last_ctx_layer
        ctx_out = i < last_ctx_layer
        mod = (jax.nn.silu(c) @ w_ada[i] + b_ada[i]).reshape(B, N_MOD, 1, D)
        if ctx_live:
            mod_c = (jax.nn.silu(c_ctx) @ w_ada[i] + b_ada[i]).reshape(1, N_MOD, 1, D)

        h = gated_residual(h, swiglu(modulate(h, norm_pre[i, 0], mod, 0), w_ffn_in[i, 0], w_ffn_out[i, 0]),
                           norm_post[i, 0], mod, 0, FFN_RES)
        if ctx_live:
            hc = gated_residual(hc, swiglu(modulate(hc, norm_pre[i, 0], mod_c, 0), w_ffn_in[i, 0], w_ffn_out[i, 0]),
                                norm_post[i, 0], mod_c, 0, FFN_RES)

        u = modulate(h, norm_pre[i, 1], mod, 1)
        uc = modulate(hc, norm_pre[i, 1], mod_c, 1) if ctx_live else None
        yc = None
        if kind == 0:
            hp = (hy_w_in[j], hy_b_in[j], hy_w_sc[j], hy_b_sc[j], hy_f_w1[j], hy_f_b1[j], hy_f_w2[j], hy_f_b2[j],
                  hy_f_w3[j], hy_f_b3[j], hy_f_w4[j], hy_f_freq[j], hy_skip[j], hy_w_out[j], hy_b_out[j])
            y = hyena_mixer(u, *hp)
            if ctx_out:
                yc = hyena_mixer(uc, *hp)
        elif kind == 1:
            y, yc = windowed_gqa(u, uc, at_w_qkv[j], at_b_qkv[j], at_sink[j], at_w_o[j], at_b_o[j],
                                 ang_row, ang_col, ctx_out)
        else:
            y = pool_mixer(u, pl_w[j], pl_b[j], pl_scale[j])
            if ctx_out:
                yc = pool_mixer(uc, pl_w[j], pl_b[j], pl_scale[j])
        h = gated_residual(h, y, norm_post[i, 1], mod, 1, 1.0)
        if ctx_out:
            hc = gated_residual(hc, yc, norm_post[i, 1], mod_c, 1, 1.0)

        h = gated_residual(h, swiglu(modulate(h, norm_pre[i, 2], mod, 2), w_ffn_in[i, 1], w_ffn_out[i, 1]),
                           norm_post[i, 2], mod, 2, FFN_RES)
        if ctx_out:
            hc = gated_residual(hc, swiglu(modulate(hc, norm_pre[i, 2], mod_c, 2), w_ffn_in[i, 1], w_ffn_out[i, 1]),
                                norm_post[i, 2], mod_c, 2, FFN_RES)
    return h
```

```python
import math
from contextlib import ExitStack
import numpy as np
import concourse.bass as bass
import concourse.mybir as mybir
from concourse.bass_utils import run_bass_kernel_spmd

F32 = mybir.dt.float32
BF16 = mybir.dt.bfloat16
AF = mybir.ActivationFunctionType
ALU = mybir.AluOpType

NCORES = 8
D = 2048
DC = D // 128
SEQ = 16384
DEPTH = 4
DFF = 5632
FC = DFF // 128
CTX = 256
EPS = 1e-6
TCORE = SEQ // NCORES

ENGS = ("tensor", "scalar", "vector", "gpsimd", "sync")
SAME_ENGINE_SYNC = True


class Res:
    __slots__ = ("name", "w", "r", "dsem", "ddep")

    def __init__(self, name=""):
        self.name = name
        self.w = None
        self.r = []
        self.dsem = None
        self.ddep = None


class Prog:
    def __init__(self, nc, stack, n_dsem=40):
        self.nc = nc
        self.esem = {e: stack.enter_context(nc.semaphore("se_" + e)) for e in ENGS[:4]}
        self.ecount = {e: 0 for e in ENGS}
        self.dsem_pool = [stack.enter_context(nc.semaphore("sd%d" % i)) for i in range(n_dsem)]
        self.dval = {id(s): 0 for s in self.dsem_pool}
        self.begin()

    def begin(self):
        self.ops = {e: [] for e in ENGS}
        self.seenE = {e: {} for e in ENGS}
        self.seenD = {e: {} for e in ENGS}
        self.free_dsems = list(self.dsem_pool)
        self.used_dsems = []

    def res(self, name=""):
        return Res(name)

    def _add_wait(self, eng, rec, d):
        if d[0] == "E":
            _, f, idx = d
            if f == eng and (eng == "tensor" or not SAME_ENGINE_SYNC):
                return
            if self.seenE[eng].get(f, -1) >= idx:
                return
            self.seenE[eng][f] = idx
            self.ops[f][idx]["sig"] = True
            rec["waits"].append(d)
        else:
            _, sem, val = d
            if self.seenD[eng].get(id(sem), -1) >= val:
                return
            self.seenD[eng][id(sem)] = val
            rec["waits"].append(d)

    def op(self, eng, fn, reads=(), writes=(), dma=None):
        rec = {"fn": fn, "waits": [], "sig": False, "dma": None}
        idx = len(self.ops[eng])
        deps = []
        for r in reads:
            if r.w is not None:
                deps.append(r.w)
        for w in writes:
            if w.w is not None:
                deps.append(w.w)
            deps.extend(w.r)
        if dma is not None:
            sres, n = dma
            if sres.dsem is None:
                sres.dsem = self.free_dsems.pop()
                self.used_dsems.append(sres.dsem)
            if sres.ddep is not None:
                deps.append(sres.ddep)
            sem = sres.dsem
            self.dval[id(sem)] += 16 * n
            ev = ("D", sem, self.dval[id(sem)])
            sres.ddep = ev
            rec["dma"] = sem
        else:
            ev = ("E", eng, idx)
        for d in deps:
            self._add_wait(eng, rec, d)
        self.ops[eng].append(rec)
        for r in reads:
            r.r.append(ev)
        for w in writes:
            w.w = ev
            w.r = []
        return ev

    def end_phase(self):
        last = {e: len(self.ops[e]) - 1 for e in ENGS[:4] if self.ops[e]}
        for e in ENGS:
            rec = {"fn": None, "waits": [], "sig": False, "dma": None}
            for f, idx in last.items():
                if f != e:
                    self._add_wait(e, rec, ("E", f, idx))
            for sem in self.used_dsems:
                self._add_wait(e, rec, ("D", sem, self.dval[id(sem)]))
            self.ops[e].append(rec)
        for e in ENGS:
            for rec in self.ops[e]:
                if rec["sig"]:
                    self.ecount[e] += 1
                    rec["sigval"] = self.ecount[e]
        ops = self.ops
        esem = self.esem
        with self.nc.Block() as block:
            for e in ENGS:
                def body(eng, e=e):
                    for rec in ops[e]:
                        for w in rec["waits"]:
                            if w[0] == "E":
                                eng.wait_ge(esem[w[1]], ops[w[1]][w[2]]["sigval"])
                            else:
                                eng.wait_ge(w[1], w[2])
                        if rec["fn"] is None:
                            continue
                        r = rec["fn"](eng)
                        if rec["dma"] is not None:
                            for ins in r:
                                ins.then_inc(rec["dma"], 16)
                        elif rec["sig"]:
                            r.then_inc(esem[e], 1)
                getattr(block, e)(body)
        self.begin()


def fm(v):
    return np.ascontiguousarray(np.asarray(v, np.float32).reshape(DC, 128).T)


class BaseCtx:
    def __init__(self, nc, p, st):
        self.nc, self.p, self.st = nc, p, st
        A = self.A
        self.sq = [A("bsq%d" % i, [128, 1024], BF16) for i in range(2)]
        self.sqR = [p.res() for i in range(2)]
        self.tmp = [A("btmp%d" % i, [128, 1024], F32) for i in range(2)]
        self.tmpR = [p.res() for i in range(2)]
        self.hres = [A("bhr%d" % i, [128, 1024], F32) for i in range(2)]
        self.hresR = [p.res() for i in range(2)]
        self.rstd = A("brstd", [128, 1024], F32)
        self.rstdR = p.res()
        self.ones = A("bones", [128, 128], BF16)
        self.onesR = p.res()
        self.epsb = A("beps", [128, 1], F32)
        self.ps = [st.enter_context(nc.psum_tensor("ps%d" % i, [128, 512], F32)) for i in range(8)]
        self.psR = [p.res("ps%d" % i) for i in range(8)]
        ones, epsb = self.ones, self.epsb
        p.op("vector", lambda e: e.memset(epsb[:], EPS), writes=[self.onesR])
        p.op("vector", lambda e: e.memset(ones[:], 1.0), writes=[self.onesR])
        self.cnt = 0

    def A(self, n, s, d):
        return self.st.enter_context(self.nc.sbuf_tensor(n, s, d))


class FFNCtx(BaseCtx):
    def __init__(self, nc, p, st):
        super().__init__(nc, p, st)
        A = self.A
        self.H = A("ffH", [128, FC * 1024], BF16)
        self.Y = A("ffY", [128, DC * 1024], F32)
        self.hidR = [p.res("hid%d" % j) for j in range(FC)]
        self.yR = [p.res("y%d" % j) for j in range(DC)]
        self.W = [A("ffW%d" % i, [128, 22 * 128], BF16) for i in range(3)]
        self.WR = [p.res("W%d" % i) for i in range(3)]
        self.sg = [A("ffsg%d" % i, [128, 1024], BF16) for i in range(2)]
        self.sgR = [p.res() for i in range(2)]
        Hf = self.H[:].bitcast(F32)
        self.stage = lambda dc, TP: (Hf[:, dc * 1024: dc * 1024 + TP], [self.hidR[2 * dc], self.hidR[2 * dc + 1]])


def load_vecs(nc, p, st, dram_aps, names):
    out = {}
    for n in names:
        t = st.enter_context(nc.sbuf_tensor("sv_" + n, [128, DC], F32))
        r = p.res("v_" + n)
        src = dram_aps[n]
        p.op("sync", lambda e, t=t, src=src: [e.dma_start(out=t[:], in_=src)], writes=[r], dma=(r, 1))
        out[n] = (t, r)
    return out


def make_mod_coefs(nc, p, st, V, pre, shift, scale, post, gate, weight, tag):
    a = st.enter_context(nc.sbuf_tensor("a_" + tag, [128, DC], F32))
    cf = st.enter_context(nc.sbuf_tensor("c_" + tag, [128, DC], F32))
    aR, cR = p.res(), p.res()
    (tp, rp), (tsc, rsc), (tpo, rpo), (tg, rg) = V[pre], V[scale], V[post], V[gate]
    p.op("vector", lambda e: e.tensor_scalar(out=a[:], in0=tsc[:], scalar1=1.0, scalar2=None, op0=ALU.add),
         reads=[rsc], writes=[aR])
    p.op("vector", lambda e: e.tensor_tensor(out=a[:], in0=a[:], in1=tp[:], op=ALU.mult),
         reads=[aR, rp], writes=[aR])
    p.op("vector", lambda e: e.tensor_scalar(out=cf[:], in0=tg[:], scalar1=float(weight), scalar2=None, op0=ALU.mult),
         reads=[rg], writes=[cR])
    p.op("vector", lambda e: e.tensor_tensor(out=cf[:], in0=cf[:], in1=tpo[:], op=ALU.mult),
         reads=[cR, rpo], writes=[cR])
    return (a, aR), V[shift], (cf, cR)


def emit_rstd(p, C, groups, psbanks):
    rstd = C.rstd
    for gi, (off, n) in enumerate(groups):
        b = psbanks[gi]
        ps = C.ps[b]
        p.op("scalar", lambda e, ps=ps, off=off, n=n: e.activation(
            out=rstd[:, off:off + n], in_=ps[:, 0:n], func=AF.Sqrt, bias=C.epsb[:, 0:1], scale=1.0 / D),
            reads=[C.psR[b], C.onesR], writes=[C.rstdR])
        p.op("vector", lambda e, off=off, n=n: e.reciprocal(
            out=rstd[:, off:off + n], in_=rstd[:, off:off + n]),
            reads=[C.rstdR], writes=[C.rstdR])


def emit_prenorm_mod(p, C, hT, t0, TP, groups, a, b, uview, uR):
    (at, aR), (bt, bR) = a, b
    for dc in range(DC):
        hv, hr = C.stage(dc, TP)
        src = hT[dc * 128:(dc + 1) * 128, t0:t0 + TP]
        p.op("sync", lambda e, hv=hv, src=src: [e.dma_start(out=hv, in_=src)], writes=hr, dma=(hr[0], 1))
        s = dc % 2
        sq = C.sq[s]
        p.op("scalar", lambda e, hv=hv, sq=sq: e.activation(out=sq[:, 0:TP], in_=hv, func=AF.Square),
             reads=hr, writes=[C.sqR[s]])
        for gi, (off, n) in enumerate(groups):
            ps = C.ps[gi]
            p.op("tensor", lambda e, ps=ps, sq=sq, off=off, n=n, dc=dc: e.matmul(
                ps[:, 0:n], C.ones[:], sq[:, off:off + n], start=(dc == 0), stop=(dc == DC - 1)),
                reads=[C.onesR, C.sqR[s]], writes=[C.psR[gi]])
    emit_rstd(p, C, groups, list(range(len(groups))))
    for dc in range(DC):
        hv, hr = C.stage(dc, TP)
        s = dc % 2
        tmp = C.tmp[s]
        p.op("vector", lambda e, hv=hv, tmp=tmp, dc=dc: e.scalar_tensor_tensor(
            out=tmp[:, 0:TP], in0=hv, scalar=at[:, dc:dc + 1], in1=C.rstd[:, 0:TP], op0=ALU.mult, op1=ALU.mult),
            reads=hr + [aR, C.rstdR], writes=[C.tmpR[s]])
        uv = uview(dc)
        p.op("scalar", lambda e, uv=uv, tmp=tmp, dc=dc: e.activation(
            out=uv, in_=tmp[:, 0:TP], func=AF.Identity, bias=bt[:, dc:dc + 1], scale=1.0),
            reads=[C.tmpR[s], bR], writes=[uR[dc]])


def emit_postnorm_residual(p, C, hT_in, hT_out, t0, TP, groups, coef, yview, yR, ssbanks, t0_out=None):
    (ct, cR) = coef
    if t0_out is None:
        t0_out = t0
    emit_rstd(p, C, groups, ssbanks)
    for dc in range(DC):
        s = dc % 2
        hr = C.hres[s]
        src = hT_in[dc * 128:(dc + 1) * 128, t0:t0 + TP]
        p.op("sync", lambda e, hr=hr, src=src: [e.dma_start(out=hr[:, 0:TP], in_=src)],
             writes=[C.hresR[s]], dma=(C.hresR[s], 1))
        tmp = C.tmp[s]
        yv = yview(dc)
        p.op("vector", lambda e, yv=yv, tmp=tmp, dc=dc: e.scalar_tensor_tensor(
            out=tmp[:, 0:TP], in0=yv, scalar=ct[:, dc:dc + 1], in1=C.rstd[:, 0:TP], op0=ALU.mult, op1=ALU.mult),
            reads=[yR[dc], cR, C.rstdR], writes=[C.tmpR[s]])
        p.op("gpsimd", lambda e, hr=hr, tmp=tmp: e.tensor_tensor(
            out=hr[:, 0:TP], in0=hr[:, 0:TP], in1=tmp[:, 0:TP], op=ALU.add),
            reads=[C.tmpR[s], C.hresR[s]], writes=[C.hresR[s]])
        dst = hT_out[dc * 128:(dc + 1) * 128, t0_out:t0_out + TP]
        p.op("sync", lambda e, hr=hr, dst=dst: [e.dma_start(out=dst, in_=hr[:, 0:TP])],
             reads=[C.hresR[s]], dma=(C.hresR[s], 1))


def emit_ffn_pass(p, C, hT_in, hT_out, w_in, w_out, t0, TP, a, b, coef):
    groups = [(o, min(512, TP - o)) for o in range(0, TP, 512)]
    NG = len(groups)
    Yb = C.Y[:].bitcast(BF16)
    uview = lambda dc: Yb[:, dc * 2048: dc * 2048 + TP]
    yview = lambda dc: C.Y[:, dc * 1024: dc * 1024 + TP]
    emit_prenorm_mod(p, C, hT_in, t0, TP, groups, a, b, uview, C.yR)

    for j in range(FC):
        par = j % 2
        wi = []
        for which in range(2):
            slot = C.cnt % 3
            C.cnt += 1
            W = C.W[slot]
            col0 = which * DFF + j * 128
            src = w_in[:, col0:col0 + 128].rearrange("(k p) c -> p k c", p=128)
            dstv = W[:, 0:16 * 128].rearrange("p (k c) -> p k c", c=128)
            p.op("gpsimd", lambda e, dstv=dstv, src=src: [e.dma_start(out=dstv, in_=src)],
                 writes=[C.WR[slot]], dma=(C.WR[slot], 1))
            wi.append((W, C.WR[slot]))
        banks = [[par * 4 + which * 2 + gi for gi in range(NG)] for which in range(2)]
        for which in range(2):
            W, WR = wi[which]
            for k in range(DC):
                uv = uview(k)
                for gi, (off, n) in enumerate(groups):
                    bk = banks[which][gi]
                    p.op("tensor", lambda e, bk=bk, W=W, k=k, uv=uv, off=off, n=n: e.matmul(
                        C.ps[bk][:, 0:n], W[:, k * 128:(k + 1) * 128], uv[:, off:off + n],
                        start=(k == 0), stop=(k == DC - 1)),
                        reads=[WR, C.yR[k]], writes=[C.psR[bk]])
        sg = C.sg[par]
        hv = C.H[:, j * 1024: j * 1024 + TP]
        for gi, (off, n) in enumerate(groups):
            bg, bu = banks[0][gi], banks[1][gi]
            p.op("scalar", lambda e, bg=bg, sg=sg, off=off, n=n: e.activation(
                out=sg[:, off:off + n], in_=C.ps[bg][:, 0:n], func=AF.Silu),
                reads=[C.psR[bg]], writes=[C.sgR[par]])
            p.op("vector", lambda e, bu=bu, sg=sg, hv=hv, off=off, n=n: e.tensor_tensor(
                out=hv[:, off:off + n], in0=C.ps[bu][:, 0:n], in1=sg[:, off:off + n], op=ALU.mult),
                reads=[C.psR[bu], C.sgR[par]], writes=[C.hidR[j]])

    ssb = [6, 7][:NG]
    for dc in range(DC):
        par = dc % 2
        wo = []
        for half in range(2):
            slot = C.cnt % 3
            C.cnt += 1
            W = C.W[slot]
            src = w_out[half * 22 * 128:(half + 1) * 22 * 128, dc * 128:(dc + 1) * 128].rearrange(
                "(k p) c -> p k c", p=128)
            dstv = W[:, 0:22 * 128].rearrange("p (k c) -> p k c", c=128)
            p.op("gpsimd", lambda e, dstv=dstv, src=src: [e.dma_start(out=dstv, in_=src)],
                 writes=[C.WR[slot]], dma=(C.WR[slot], 1))
            wo.append((W, C.WR[slot]))
        banks = [par * 2 + gi for gi in range(NG)]
        for j in range(FC):
            W, WR = wo[j // 22]
            jj = j % 22
            hv = C.H[:, j * 1024: j * 1024 + TP]
            for gi, (off, n) in enumerate(groups):
                bk = banks[gi]
                p.op("tensor", lambda e, bk=bk, W=W, jj=jj, hv=hv, off=off, n=n, j=j: e.matmul(
                    C.ps[bk][:, 0:n], W[:, jj * 128:(jj + 1) * 128], hv[:, off:off + n],
                    start=(j == 0), stop=(j == FC - 1)),
                    reads=[WR, C.hidR[j]], writes=[C.psR[bk]])
        yv = yview(dc)
        s = dc % 2
        sq = C.sq[s]
        for gi, (off, n) in enumerate(groups):
            bk = banks[gi]
            p.op("scalar", lambda e, bk=bk, yv=yv, off=off, n=n: e.activation(
                out=yv[:, off:off + n], in_=C.ps[bk][:, 0:n], func=AF.Copy),
                reads=[C.psR[bk]], writes=[C.yR[dc]])
            p.op("scalar", lambda e, bk=bk, sq=sq, off=off, n=n: e.activation(
                out=sq[:, off:off + n], in_=C.ps[bk][:, 0:n], func=AF.Square),
                reads=[C.psR[bk]], writes=[C.sqR[s]])
        for gi, (off, n) in enumerate(groups):
            sb = ssb[gi]
            p.op("tensor", lambda e, sb=sb, sq=sq, off=off, n=n, dc=dc: e.matmul(
                C.ps[sb][:, 0:n], C.ones[:], sq[:, off:off + n], start=(dc == 0), stop=(dc == DC - 1)),
                reads=[C.onesR, C.sqR[s]], writes=[C.psR[sb]])
    emit_postnorm_residual(p, C, hT_in, hT_out, t0, TP, groups, coef, yview, C.yR, ssb)


VEC_NAMES = ("pre", "post", "shift", "scale", "gate")


def build_ffn_launch(tc, with_ctx, tcx=0):
    nc = bass.Bass("TRN2", target_bir_lowering=False)
    dt = lambda n, s, k: nc.dram_tensor(n, s, F32, kind=k).ap()
    hT = dt("hT", [D, tc], "ExternalInput")
    hTo = dt("hTo", [D, tc], "ExternalOutput")
    w_in = dt("w_in", [D, 2 * DFF], "ExternalInput")
    w_out = dt("w_out", [DFF, D], "ExternalInput")
    vec = {n: dt("v_" + n, [128, DC], "ExternalInput") for n in VEC_NAMES}
    if with_ctx:
        cT = dt("cT", [D, tcx], "ExternalInput")
        cTo = dt("cTo", [D, tcx], "ExternalOutput")
        vecc = {n + "_c": dt("v_" + n + "_c", [128, DC], "ExternalInput") for n in ("shift", "scale", "gate")}
    with ExitStack() as st:
        p = Prog(nc, st)
        C = FFNCtx(nc, p, st)
        V = load_vecs(nc, p, st, vec, VEC_NAMES)
        a, b, coef = make_mod_coefs(nc, p, st, V, "pre", "shift", "scale", "post", "gate", 0.5, "m")
        for t0 in range(0, tc, 1024):
            emit_ffn_pass(p, C, hT, hTo, w_in, w_out, t0, min(1024, tc - t0), a, b, coef)
        if with_ctx:
            V.update(load_vecs(nc, p, st, vecc, list(vecc)))
            a2, b2, coef2 = make_mod_coefs(nc, p, st, V, "pre", "shift_c", "scale_c", "post", "gate_c", 0.5, "c")
            emit_ffn_pass(p, C, cT, cTo, w_in, w_out, 0, tcx, a2, b2, coef2)
        p.end_phase()
    return nc


MODC = 9 * D // NCORES


def build_mod_launch():
    nc = bass.Bass("TRN2", target_bir_lowering=False)
    dt = lambda n, s, k: nc.dram_tensor(n, s, F32, kind=k).ap()
    cc = dt("cc", [128, DC * 2], "ExternalInput")
    w = dt("w", [DEPTH, D, MODC], "ExternalInput")
    bb = dt("bb", [DEPTH, 2, MODC], "ExternalInput")
    out = dt("out", [DEPTH, 2, MODC], "ExternalOutput")
    blocks = [(o, min(512, MODC - o)) for o in range(0, MODC, 512)]
    with ExitStack() as st:
        p = Prog(nc, st)
        A = lambda n, s, d: st.enter_context(nc.sbuf_tensor(n, s, d))
        sc = A("sc", [128, DC * 2], F32)
        scR = p.res()
        p.op("sync", lambda e: [e.dma_start(out=sc[:], in_=cc)], writes=[scR], dma=(scR, 1))
        p.op("scalar", lambda e: e.activation(out=sc[:], in_=sc[:], func=AF.Silu), reads=[scR], writes=[scR])
        W = [A("mw%d" % i, [128, MODC], F32) for i in range(3)]
        WR = [p.res() for i in range(3)]
        ps = [st.enter_context(nc.psum_tensor("ps%d" % i, [128, 512], F32)) for i in range(8)]
        psR = [p.res() for i in range(8)]
        bt = A("mb", [2, MODC], F32)
        btR = p.res()
        ot = A("mo", [2, MODC], F32)
        otR = p.res()
        cnt = 0
        for i in range(DEPTH):
            p.op("sync", lambda e, i=i: [e.dma_start(out=bt[:], in_=bb[i])], writes=[btR], dma=(btR, 1))
            for k in range(DC):
                s = cnt % 3
                cnt += 1
                Wt = W[s]
                src = w[i, k * 128:(k + 1) * 128, :]
                p.op("sync", lambda e, Wt=Wt, src=src: [e.dma_start(out=Wt[:], in_=src)], writes=[WR[s]], dma=(WR[s], 1))
                for bi, (o, n) in enumerate(blocks):
                    p.op("tensor", lambda e, bi=bi, Wt=Wt, o=o, n=n, k=k: e.matmul(
                        ps[bi][0:2, 0:n], sc[:, 2 * k:2 * k + 2], Wt[:, o:o + n], start=(k == 0), stop=(k == DC - 1)),
                        reads=[scR, WR[s]], writes=[psR[bi]])
            for bi, (o, n) in enumerate(blocks):
                p.op("vector", lambda e, bi=bi, o=o, n=n: e.tensor_tensor(
                    out=ot[:, o:o + n], in0=ps[bi][0:2, 0:n], in1=bt[:, o:o + n], op=ALU.add),
                    reads=[psR[bi], btR], writes=[otR])
            p.op("sync", lambda e, i=i: [e.dma_start(out=out[i], in_=ot[:])], reads=[otR], dma=(otR, 1))
        p.end_phase()
    return nc


def run_mod(c, c_ctx, w_ada, b_ada):
    nc = build_mod_launch()
    cc = np.stack([np.asarray(c, np.float32).reshape(D), np.asarray(c_ctx, np.float32).reshape(D)], 0)
    cc = np.ascontiguousarray(cc.reshape(2, DC, 128).transpose(2, 1, 0).reshape(128, DC * 2))
    in_maps = []
    for core in range(NCORES):
        sl = slice(core * MODC, (core + 1) * MODC)
        bsl = np.asarray(b_ada[:, sl], np.float32)
        in_maps.append({"cc": cc, "w": np.ascontiguousarray(w_ada[:, :, sl]),
                        "bb": np.ascontiguousarray(np.stack([bsl, bsl], 1))})
    res = run_bass_kernel_spmd(nc, in_maps, core_ids=list(range(NCORES)))
    mod = np.concatenate([r["out"] for r in res.results], axis=2)
    return mod.reshape(DEPTH, 2, 9, D)


HD = 128
NH = 16
NKV = 4
QD = NH * HD
KD = NKV * HD
ATQ = 512
AXT = ATQ + 256
ANT = AXT + CTX
ASCALE = HD ** -0.5


def swap_cols(w):
    w = np.asarray(w)
    sh = w.shape
    w = w.reshape(sh[:-1] + (2, 2, 32))[..., ::-1, :]
    return np.ascontiguousarray(w.reshape(sh))


class AttnCtx(BaseCtx):
    def __init__(self, nc, p, st):
        super().__init__(nc, p, st)
        A = self.A
        self.Y = A("atY", [128, DC * 512], F32)
        self.yR = [p.res() for _ in range(DC)]
        self.U = A("atU", [128, DC * ANT], BF16)
        self.uR = [p.res() for _ in range(DC)]
        self.KT = A("atK", [128, NKV * ANT], BF16)
        self.kR = [p.res() for _ in range(NKV)]
        self.V = A("atV", [128, 8 * 512], BF16)
        self.vR = [p.res() for _ in range(8)]
        self.QT = [A("atQ%d" % i, [128, 4 * 512], BF16) for i in range(2)]
        self.qR = [p.res() for _ in range(2)]
        self.OT = A("atO", [128, NH * ATQ], BF16)
        self.oR = [p.res() for _ in range(NH)]
        self.PT = [A("atP%d" % i, [128, 5 * 512], BF16) for i in range(2)]
        self.pR = [[p.res() for _ in range(5)] for _ in range(2)]
        self.W = [A("atW%d" % i, [128, 16 * 128], BF16) for i in range(4)]
        self.WR = [p.res() for _ in range(4)]
        self.WV = A("atWV", [128, 16 * 512], BF16)
        self.WVR = p.res()
        self.Ct = A("atC", [128, AXT], F32)
        self.St = A("atS", [128, AXT], F32)
        self.tabR = p.res()
        self.t1 = [A("att1%d" % i, [128, 512], F32) for i in range(2)]
        self.t2 = [A("att2%d" % i, [128, 512], F32) for i in range(2)]
        self.t1R = [p.res() for _ in range(2)]
        self.t2R = [p.res() for _ in range(2)]
        self.rden = [A("atrd%d" % i, [128, 512], F32) for i in range(2)]
        self.rdR = [p.res() for _ in range(2)]
        self.masks = A("atM", [128, 4 * 128], BF16)
        self.mR = p.res()
        self.small = A("atsm", [128, 16 * 4 + 4 * 2], F32)
        self.smR = p.res()
        self.bv = A("atbv", [1, 512], BF16)
        self.bvR = p.res()
        self.stage = lambda dc, TP: (self.Y[:, dc * 512: dc * 512 + TP], [self.yR[dc]])
        self.rcnt = 0


def emit_rope(p, C, psq, psw, bq, bsw, c0, n, outv, outR, tag_reads):
    s = C.rcnt % 2
    C.rcnt += 1
    t1, t2 = C.t1[s], C.t2[s]
    p.op("vector", lambda e: e.scalar_tensor_tensor(
        out=t1[:, 0:n], in0=C.ps[psq][:, 0:n], scalar=bq, in1=C.Ct[:, c0:c0 + n], op0=ALU.add, op1=ALU.mult),
        reads=[C.psR[psq], C.smR, C.tabR], writes=[C.t1R[s]])
    p.op("vector", lambda e: e.scalar_tensor_tensor(
        out=t2[:, 0:n], in0=C.ps[psw][:, 0:n], scalar=bsw, in1=C.St[:, c0:c0 + n], op0=ALU.add, op1=ALU.mult),
        reads=[C.psR[psw], C.smR, C.tabR], writes=[C.t2R[s]])
    return t1, t2, s


def emit_attn_pass(p, C, pi, hTh, hcT, hTo, w_qkv, w_o, ctab, stab, masks_fl, a, b, a_c, b_c, coef):
    c0 = pi * ATQ
    SM = C.small
    bq = lambda h: SM[:, h:h + 1]
    bqs = lambda h: SM[:, 16 + h:17 + h]
    bo = lambda dc: SM[:, 32 + dc:33 + dc]
    esk = lambda h: SM[:, 48 + h:49 + h]
    bk = lambda kv: SM[:, 64 + kv:65 + kv]
    bks = lambda kv: SM[:, 68 + kv:69 + kv]
    p.op("sync", lambda e: [e.dma_start(out=C.Ct[:], in_=ctab[:, c0:c0 + AXT]),
                            e.dma_start(out=C.St[:], in_=stab[:, c0:c0 + AXT])], writes=[C.tabR], dma=(C.tabR, 2))
    p.op("gpsimd", lambda e: [e.dma_start(out=C.masks[:, 256:512], in_=masks_fl[pi])], writes=[C.mR], dma=(C.mR, 1))
    uv = lambda o: (lambda dc: C.U[:, dc * ANT + o: dc * ANT + o + 512])
    emit_prenorm_mod(p, C, hTh, c0, 512, [(0, 512)], a, b, uv(0), C.uR)
    uv2 = lambda dc: C.U[:, dc * ANT + 512: dc * ANT + 768]
    emit_prenorm_mod(p, C, hTh, c0 + 512, 256, [(0, 256)], a, b, uv2, C.uR)
    uv3 = lambda dc: C.U[:, dc * ANT + 768: dc * ANT + 1024]
    emit_prenorm_mod(p, C, hcT, 0, 256, [(0, 256)], a_c, b_c, uv3, C.uR)
    U = lambda k, o, n: C.U[:, k * ANT + o: k * ANT + o + n]

    def load_w(col0, swapped):
        slot = C.cnt % 4
        C.cnt += 1
        W = C.W[slot]
        dstv = W[:].rearrange("p (k a b r) -> p k a b r", k=16, a=2, b=2, r=32)
        srcv = w_qkv[:, col0:col0 + 128].rearrange("(k p) (a b r) -> p k a b r", p=128, a=2, b=2, r=32)
        if not swapped:
            p.op("gpsimd", lambda e: [e.dma_start(out=W[:].rearrange("p (k c) -> p k c", c=128),
                                                  in_=w_qkv[:, col0:col0 + 128].rearrange("(k p) c -> p k c", p=128))],
                 writes=[C.WR[slot]], dma=(C.WR[slot], 1))
        else:
            p.op("gpsimd", lambda e: [e.dma_start(out=dstv[:, :, aa, 0, :], in_=srcv[:, :, aa, 1, :]) for aa in range(2)] +
                 [e.dma_start(out=dstv[:, :, aa, 1, :], in_=srcv[:, :, aa, 0, :]) for aa in range(2)],
                 writes=[C.WR[slot]], dma=(C.WR[slot], 4))
        return W, C.WR[slot]

    for kv in range(NKV):
        Wk, WkR = load_w(QD + kv * 128, False)
        Ws, WsR = load_w(QD + kv * 128, True)
        for tg in range(2):
            b0, b1 = (kv % 2) * 4 + tg * 2, (kv % 2) * 4 + tg * 2 + 1
            for (W, WR, bkx) in ((Wk, WkR, b0), (Ws, WsR, b1)):
                for k in range(DC):
                    p.op("tensor", lambda e, W=W, k=k, bkx=bkx, tg=tg: e.matmul(
                        C.ps[bkx][:, :], W[:, k * 128:(k + 1) * 128], U(k, tg * 512, 512), start=(k == 0), stop=(k == DC - 1)),
                        reads=[WR, C.uR[k]], writes=[C.psR[bkx]])
            nrope = 512 if tg == 0 else 256
            kout = C.KT[:, kv * ANT + tg * 512: kv * ANT + tg * 512 + nrope]
            t1, t2, s = emit_rope(p, C, b0, b1, bk(kv), bks(kv), tg * 512, nrope, kout, C.kR[kv], None)
            p.op("gpsimd", lambda e, t1=t1, t2=t2, kout=kout, nrope=nrope: e.tensor_tensor(
                out=kout, in0=t1[:, 0:nrope], in1=t2[:, 0:nrope], op=ALU.add),
                reads=[C.t1R[s], C.t2R[s]], writes=[C.kR[kv]])
            if tg == 1:
                kc = C.KT[:, kv * ANT + 768: kv * ANT + 1024]
                p.op("scalar", lambda e, kc=kc, b0=b0, kv=kv: e.activation(
                    out=kc, in_=C.ps[b0][:, 256:512], func=AF.Identity, bias=bk(kv), scale=1.0),
                    reads=[C.psR[b0], C.smR], writes=[C.kR[kv]])
    for blk in range(8):
        bkx = blk % 2
        for k in range(DC):
            p.op("tensor", lambda e, k=k, bkx=bkx, blk=blk: e.matmul(
                C.ps[bkx][:, :], U(k, blk * 128, 128), C.WV[:, k * 512:(k + 1) * 512], start=(k == 0), stop=False),
                reads=[C.WVR, C.uR[k]], writes=[C.psR[bkx]])
        p.op("tensor", lambda e, bkx=bkx: e.matmul(C.ps[bkx][:, :], C.ones[0:1, :], C.bv[:, :], start=False, stop=True),
             reads=[C.onesR, C.bvR], writes=[C.psR[bkx]])
        p.op("scalar", lambda e, bkx=bkx, blk=blk: e.activation(
            out=C.V[:, blk * 512:(blk + 1) * 512], in_=C.ps[bkx][:, :], func=AF.Copy),
            reads=[C.psR[bkx]], writes=[C.vR[blk]])
    for g in range(NKV):
        QT, qR = C.QT[g % 2], C.qR[g % 2]
        QTv = QT[:].rearrange("p (n j t) -> p n j t", n=4, j=4, t=128)
        for j in range(4):
            h = 4 * g + j
            Wq, WqR = load_w(h * 128, False)
            Ws, WsR = load_w(h * 128, True)
            b0, b1 = (j % 2) * 2, (j % 2) * 2 + 1
            for (W, WR, bkx) in ((Wq, WqR, b0), (Ws, WsR, b1)):
                for k in range(DC):
                    p.op("tensor", lambda e, W=W, k=k, bkx=bkx: e.matmul(
                        C.ps[bkx][:, :], W[:, k * 128:(k + 1) * 128], U(k, 128, 512), start=(k == 0), stop=(k == DC - 1)),
                        reads=[WR, C.uR[k]], writes=[C.psR[bkx]])
            t1, t2, s = emit_rope(p, C, b0, b1, bq(h), bqs(h), 128, 512, None, None, None)
            p.op("gpsimd", lambda e, t1=t1, t2=t2, j=j, QTv=QTv: e.tensor_tensor(
                out=QTv[:, :, j, :], in0=t1[:].rearrange("p (n t) -> p n t", t=128),
                in1=t2[:].rearrange("p (n t) -> p n t", t=128), op=ALU.add),
                reads=[C.t1R[s], C.t2R[s]], writes=[qR])
        for n in range(4):
            par = (g * 4 + n) % 2
            PT, pR = C.PT[par], C.pR[par]
            kcols = [(n + i) * 128 for i in range(3)] + [768, 896]
            vblk = [n, n + 1, n + 2, 6, 7]
            for kb in range(5):
                p.op("tensor", lambda e, kb=kb, n=n, g=g, QT=QT, kc=kcols[kb]: e.matmul(
                    C.ps[kb][:, :], C.KT[:, g * ANT + kc: g * ANT + kc + 128], QT[:, n * 512:(n + 1) * 512],
                    start=True, stop=True), reads=[C.kR[g], qR], writes=[C.psR[kb]])
                p.op("scalar", lambda e, kb=kb, PT=PT: e.activation(
                    out=PT[:, kb * 512:(kb + 1) * 512], in_=C.ps[kb][:, :], func=AF.Exp, scale=ASCALE),
                    reads=[C.psR[kb]], writes=[pR[kb]])
                if kb in (0, 2):
                    if kb == 0:
                        mo = 256 if n == 0 else 0
                    else:
                        mo = 384 if n == 3 else 128
                    PTv = PT[:, kb * 512:(kb + 1) * 512].rearrange("p (j t) -> p j t", t=128)
                    mv = C.masks[:, mo:mo + 128].unsqueeze(1).broadcast_to([128, 4, 128])
                    p.op("vector", lambda e, PTv=PTv, mv=mv: e.tensor_tensor(out=PTv, in0=PTv, in1=mv, op=ALU.mult),
                         reads=[pR[kb], C.mR], writes=[pR[kb]])
            for kb in range(5):
                p.op("tensor", lambda e, kb=kb, g=g, PT=PT, vb=vblk[kb]: e.matmul(
                    C.ps[5][:, :], C.V[:, vb * 512 + g * 128: vb * 512 + (g + 1) * 128], PT[:, kb * 512:(kb + 1) * 512],
                    start=(kb == 0), stop=(kb == 4)), reads=[C.vR[vblk[kb]], pR[kb]], writes=[C.psR[5]])
                p.op("tensor", lambda e, kb=kb, PT=PT: e.matmul(
                    C.ps[6][:, :], C.ones[:], PT[:, kb * 512:(kb + 1) * 512], start=(kb == 0), stop=(kb == 4)),
                    reads=[C.onesR, pR[kb]], writes=[C.psR[6]])
            rd, rdR = C.rden[par], C.rdR[par]
            for j in range(4):
                p.op("vector", lambda e, j=j, rd=rd, g=g: e.tensor_scalar(
                    out=rd[:, j * 128:(j + 1) * 128], in0=C.ps[6][:, j * 128:(j + 1) * 128], scalar1=esk(4 * g + j),
                    scalar2=None, op0=ALU.add), reads=[C.psR[6], C.smR], writes=[rdR])
            p.op("vector", lambda e, rd=rd: e.reciprocal(out=rd[:], in_=rd[:]), reads=[rdR], writes=[rdR])
            OTv = C.OT[:].rearrange("p (h t) -> p h t", h=NH)[:, 4 * g:4 * g + 4, n * 128:(n + 1) * 128]
            p.op("vector", lambda e, rd=rd, OTv=OTv: e.tensor_tensor(
                out=OTv, in0=C.ps[5][:].rearrange("p (j t) -> p j t", t=128),
                in1=rd[:].rearrange("p (j t) -> p j t", t=128), op=ALU.mult),
                reads=[C.psR[5], rdR], writes=[C.oR[4 * g + jj] for jj in range(4)])
    for dc in range(DC):
        slot = C.cnt % 4
        C.cnt += 1
        W = C.W[slot]
        p.op("gpsimd", lambda e, W=W, dc=dc: [e.dma_start(
            out=W[:].rearrange("p (h c) -> p h c", c=128),
            in_=w_o[:, dc * 128:(dc + 1) * 128].rearrange("(h p) c -> p h c", p=128))],
            writes=[C.WR[slot]], dma=(C.WR[slot], 1))
        bkx = dc % 2
        for h in range(NH):
            p.op("tensor", lambda e, W=W, h=h, bkx=bkx: e.matmul(
                C.ps[bkx][:, :], W[:, h * 128:(h + 1) * 128], C.OT[:, h * ATQ:(h + 1) * ATQ], start=(h == 0), stop=(h == NH - 1)),
                reads=[C.WR[slot], C.oR[h]], writes=[C.psR[bkx]])
        s = dc % 2
        p.op("scalar", lambda e, dc=dc, bkx=bkx: e.activation(
            out=C.Y[:, dc * 512:(dc + 1) * 512], in_=C.ps[bkx][:, :], func=AF.Identity, bias=bo(dc), scale=1.0),
            reads=[C.psR[bkx], C.smR], writes=[C.yR[dc]])
        p.op("scalar", lambda e, dc=dc, bkx=bkx, s=s: e.activation(
            out=C.sq[s][:, 0:512], in_=C.ps[bkx][:, :], func=AF.Square, bias=bo(dc), scale=1.0),
            reads=[C.psR[bkx], C.smR], writes=[C.sqR[s]])
        p.op("tensor", lambda e, dc=dc, s=s: e.matmul(
            C.ps[7][:, :], C.ones[:], C.sq[s][:, 0:512], start=(dc == 0), stop=(dc == DC - 1)),
            reads=[C.onesR, C.sqR[s]], writes=[C.psR[7]])
    yview = lambda dc: C.Y[:, dc * 512:(dc + 1) * 512]
    emit_postnorm_residual(p, C, hTh, hTo, c0 + 128, ATQ, [(0, ATQ)], coef, yview, C.yR, [7], t0_out=pi * ATQ)


def build_attn_launch(npass):
    nc = bass.Bass("TRN2", target_bir_lowering=False)
    dt = lambda n, s, k: nc.dram_tensor(n, s, F32, kind=k).ap()
    tcore = npass * ATQ
    hTh = dt("hTh", [D, tcore + 256], "ExternalInput")
    hcT = dt("hcT", [D, CTX], "ExternalInput")
    hTo = dt("hTo", [D, tcore], "ExternalOutput")
    w_qkv = dt("w_qkv", [D, QD + 2 * KD], "ExternalInput")
    w_o = dt("w_o", [D, D], "ExternalInput")
    ctab = dt("ctab", [128, tcore + 256], "ExternalInput")
    stab = dt("stab", [128, tcore + 256], "ExternalInput")
    masks_fl = dt("masks_fl", [npass, 128, 256], "ExternalInput")
    tri = dt("tri", [128, 256], "ExternalInput")
    small = dt("small", [128, 72], "ExternalInput")
    bvrow = dt("bvrow", [1, 512], "ExternalInput")
    names = ("pre", "post", "shift", "scale", "gate", "shift_c", "scale_c")
    vec = {n: dt("v_" + n, [128, DC], "ExternalInput") for n in names}
    with ExitStack() as st:
        p = Prog(nc, st)
        C = AttnCtx(nc, p, st)
        V = load_vecs(nc, p, st, vec, names)
        a, b, coef = make_mod_coefs(nc, p, st, V, "pre", "shift", "scale", "post", "gate", 1.0, "m")
        a_c, b_c, _ = make_mod_coefs(nc, p, st, V, "pre", "shift_c", "scale_c", "post", "gate", 1.0, "c")
        p.op("gpsimd", lambda e: [e.dma_start(out=C.masks[:, 0:256], in_=tri)], writes=[C.mR], dma=(C.mR, 1))
        p.op("sync", lambda e: [e.dma_start(out=C.small[:], in_=small)], writes=[C.smR], dma=(C.smR, 1))
        p.op("scalar", lambda e: e.activation(out=C.small[:, 48:64], in_=C.small[:, 48:64], func=AF.Exp),
             reads=[C.smR], writes=[C.smR])
        p.op("gpsimd", lambda e: [e.dma_start(out=C.bv[:], in_=bvrow)], writes=[C.bvR], dma=(C.bvR, 1))
        p.op("gpsimd", lambda e: [e.dma_start(
            out=C.WV[:].rearrange("p (k c) -> p k c", c=512),
            in_=w_qkv[:, QD + KD:QD + 2 * KD].rearrange("(k p) c -> p k c", p=128))], writes=[C.WVR], dma=(C.WVR, 1))
        for pi in range(npass):
            emit_attn_pass(p, C, pi, hTh, hcT, hTo, w_qkv, w_o, ctab, stab, masks_fl, a, b, a_c, b_c, coef)
        p.end_phase()
    return nc


def rope_tables(tok0, n):
    t = np.arange(tok0, tok0 + n)
    inv = (10000.0 ** (-np.arange(32, dtype=np.float32) / 32)).astype(np.float32)
    pr = (t // 64).astype(np.float32)
    pc = (t % 64).astype(np.float32)
    ang_r = (pr[None, :] * inv[:, None]).astype(np.float32)
    ang_c = (pc[None, :] * inv[:, None]).astype(np.float32)
    C = np.zeros((128, n), np.float32)
    S = np.zeros((128, n), np.float32)
    for blk, ang in ((0, ang_r), (64, ang_c)):
        C[blk:blk + 32] = np.cos(ang)
        C[blk + 32:blk + 64] = np.cos(ang)
        S[blk:blk + 32] = -np.sin(ang)
        S[blk + 32:blk + 64] = np.sin(ang)
    return C, S


def attn_inputs(core, npass, hT_full, hcT, w_qkv, b_qkv, sink, w_o, b_o, vecs, ntok_total):
    tcore = npass * ATQ
    t0 = core * tcore
    hTh = np.zeros((D, tcore + 256), np.float32)
    lo, hi = max(0, t0 - 128), min(ntok_total, t0 + tcore + 128)
    hTh[:, lo - (t0 - 128):hi - (t0 - 128)] = hT_full[:, lo:hi]
    ctab, stab = rope_tables(t0 - 128, tcore + 256)
    j = np.arange(128)[:, None]
    i = np.arange(128)[None, :]
    tri_prev = (j >= i).astype(np.float32)
    tri_next = (j <= i).astype(np.float32)
    tri = np.concatenate([tri_prev, tri_next], 1)
    mfl = np.zeros((npass, 128, 256), np.float32)
    for pi in range(npass):
        first = (t0 + pi * ATQ == 0)
        last = (t0 + (pi + 1) * ATQ == ntok_total)
        mfl[pi, :, :128] = 0.0 if first else tri_prev
        mfl[pi, :, 128:] = 0.0 if last else tri_next
    small = np.zeros((128, 72), np.float32)
    bq = np.asarray(b_qkv[:QD], np.float32).reshape(NH, 128)
    bk = np.asarray(b_qkv[QD:QD + KD], np.float32).reshape(NKV, 128)
    small[:, 0:16] = bq.T
    small[:, 16:32] = swap_cols(bq).T
    small[:, 32:48] = fm(b_o)
    small[:, 48:64] = np.asarray(sink, np.float32)[None, :]
    small[:, 64:68] = bk.T
    small[:, 68:72] = swap_cols(bk).T
    m = {"hTh": hTh, "hcT": np.ascontiguousarray(hcT), "w_qkv": w_qkv, "w_o": w_o, "ctab": ctab, "stab": stab,
         "masks_fl": mfl, "tri": tri, "small": small,
         "bvrow": np.ascontiguousarray(np.asarray(b_qkv[QD + KD:], np.float32)[None, :])}
    for k, v in vecs.items():
        m["v_" + k] = fm(v)
    return m


PTQ = 512
PHW = PTQ + 16


class PoolCtx(BaseCtx):
    def __init__(self, nc, p, st):
        super().__init__(nc, p, st)
        A = self.A
        self.Y = A("plY", [128, DC * 528], F32)
        self.yR = [p.res() for _ in range(DC)]
        self.U = A("plU", [128, DC * PHW], F32)
        self.uR = [p.res() for _ in range(DC)]
        self.S = [A("plS%d" % i, [128, PHW], F32) for i in range(4)]
        self.sR = [p.res() for _ in range(4)]
        self.PA = A("plPA", [128, DC * PTQ], BF16)
        self.paR = [p.res() for _ in range(DC)]
        self.Wp = A("plW", [128, 4 * 4 * 512], BF16)
        self.WpR = p.res()
        self.vm = A("plvm", [128, PHW], F32)
        self.rc = A("plrc", [128, 4 * PTQ], F32)
        self.cR = p.res()
        self.sb = A("plsb", [128, 3 * DC], F32)
        self.sbR = p.res()
        self.stage = lambda dc, TP: (self.Y[:, dc * 528: dc * 528 + TP], [self.yR[dc]])


def emit_pool_pass(p, C, pi, hTh, hTo, vmask, rcnt, a, b, coef):
    c0 = pi * PTQ
    p.op("sync", lambda e: [e.dma_start(out=C.vm[:], in_=vmask[pi]), e.dma_start(out=C.rc[:], in_=rcnt[pi])],
         writes=[C.cR], dma=(C.cR, 2))
    uv = lambda dc: C.U[:, dc * PHW:(dc + 1) * PHW]
    emit_prenorm_mod(p, C, hTh, c0, PHW, [(0, 512), (512, 16)], a, b, uv, C.uR)
    for dc in range(DC):
        g = dc // 4
        u = uv(dc)
        p.op("vector", lambda e, u=u: e.tensor_tensor(out=u, in0=u, in1=C.vm[:], op=ALU.mult),
             reads=[C.uR[dc], C.cR], writes=[C.uR[dc]])
        cur, curR = u, C.uR[dc]
        lo, hi = 0, PHW
        for lvl in range(g + 1):
            sh = 1 if lvl < 2 else 2 ** (lvl - 1)
            s = (dc * 4 + lvl) % 4
            S, SR = C.S[s], C.sR[s]
            if lvl == 0:
                nlo, nhi = lo + 1, hi
                in0, in1 = cur[:, nlo - 1:nhi - 1], cur[:, nlo:nhi]
            else:
                nlo, nhi = lo + sh, hi - sh
                in0, in1 = cur[:, nlo - sh:nhi - sh], cur[:, nlo + sh:nhi + sh]
            p.op("vector", lambda e, S=S, in0=in0, in1=in1, nlo=nlo, nhi=nhi: e.tensor_tensor(
                out=S[:, nlo:nhi], in0=in0, in1=in1, op=ALU.add), reads=[curR], writes=[SR])
            cur, curR, lo, hi = S, SR, nlo, nhi
        s2 = (dc * 4 + g + 1) % 4
        T, TR = C.S[s2], C.sR[s2]
        p.op("vector", lambda e, T=T, cur=cur, g=g: e.tensor_tensor(
            out=T[:, 8:8 + PTQ], in0=cur[:, 8:8 + PTQ], in1=C.rc[:, g * PTQ:(g + 1) * PTQ], op=ALU.mult),
            reads=[curR, C.cR], writes=[TR])
        p.op("gpsimd", lambda e, T=T, u=u, dc=dc: e.tensor_tensor(
            out=C.PA[:, dc * PTQ:(dc + 1) * PTQ], in0=T[:, 8:8 + PTQ], in1=u[:, 8:8 + PTQ], op=ALU.subtract),
            reads=[TR, C.uR[dc]], writes=[C.paR[dc]])
    for dc in range(DC):
        g, cc = dc // 4, dc % 4
        bkx = dc % 2
        for k in range(4):
            p.op("tensor", lambda e, g=g, cc=cc, k=k, bkx=bkx: e.matmul(
                C.ps[bkx][:, :], C.Wp[:, (g * 4 + k) * 512 + cc * 128:(g * 4 + k) * 512 + (cc + 1) * 128],
                C.PA[:, (4 * g + k) * PTQ:(4 * g + k + 1) * PTQ], start=(k == 0), stop=(k == 3)),
                reads=[C.WpR, C.paR[4 * g + k]], writes=[C.psR[bkx]])
        s = dc % 2
        p.op("scalar", lambda e, dc=dc, bkx=bkx: e.activation(
            out=C.Y[:, dc * 528: dc * 528 + PTQ], in_=C.ps[bkx][:, :], func=AF.Identity,
            bias=C.sb[:, 2 * DC + dc:2 * DC + dc + 1], scale=C.sb[:, dc:dc + 1]),
            reads=[C.psR[bkx], C.sbR], writes=[C.yR[dc]])
        p.op("scalar", lambda e, dc=dc, bkx=bkx, s=s: e.activation(
            out=C.sq[s][:, 0:PTQ], in_=C.ps[bkx][:, :], func=AF.Square,
            bias=C.sb[:, 2 * DC + dc:2 * DC + dc + 1], scale=C.sb[:, dc:dc + 1]),
            reads=[C.psR[bkx], C.sbR], writes=[C.sqR[s]])
        p.op("tensor", lambda e, dc=dc, s=s: e.matmul(
            C.ps[7][:, :], C.ones[:], C.sq[s][:, 0:PTQ], start=(dc == 0), stop=(dc == DC - 1)),
            reads=[C.onesR, C.sqR[s]], writes=[C.psR[7]])
    yview = lambda dc: C.Y[:, dc * 528: dc * 528 + PTQ]
    emit_postnorm_residual(p, C, hTh, hTo, c0 + 8, PTQ, [(0, PTQ)], coef, yview, C.yR, [7], t0_out=pi * PTQ)


def build_pool_launch(npass):
    nc = bass.Bass("TRN2", target_bir_lowering=False)
    dt = lambda n, s, k: nc.dram_tensor(n, s, F32, kind=k).ap()
    tcore = npass * PTQ
    hTh = dt("hTh", [D, tcore + 16], "ExternalInput")
    hTo = dt("hTo", [D, tcore], "ExternalOutput")
    plw = dt("plw", [4, 512, 512], "ExternalInput")
    vmask = dt("vmask", [npass, 128, PHW], "ExternalInput")
    rcnt = dt("rcnt", [npass, 128, 4 * PTQ], "ExternalInput")
    sbv = dt("sbv", [128, 2 * DC], "ExternalInput")
    names = ("pre", "post", "shift", "scale", "gate")
    vec = {n: dt("v_" + n, [128, DC], "ExternalInput") for n in names}
    with ExitStack() as st:
        p = Prog(nc, st)
        C = PoolCtx(nc, p, st)
        V = load_vecs(nc, p, st, vec, names)
        a, b, coef = make_mod_coefs(nc, p, st, V, "pre", "shift", "scale", "post", "gate", 1.0, "m")
        p.op("sync", lambda e: [e.dma_start(out=C.sb[:, 0:2 * DC], in_=sbv)], writes=[C.sbR], dma=(C.sbR, 1))
        p.op("vector", lambda e: e.tensor_tensor(out=C.sb[:, 2 * DC:3 * DC], in0=C.sb[:, 0:DC], in1=C.sb[:, DC:2 * DC],
                                                 op=ALU.mult), reads=[C.sbR], writes=[C.sbR])
        p.op("gpsimd", lambda e: [e.dma_start(
            out=C.Wp[:].rearrange("p (g k c) -> p g k c", g=4, k=4),
            in_=plw.rearrange("g (k p) c -> p g k c", p=128))], writes=[C.WpR], dma=(C.WpR, 1))
        for pi in range(npass):
            emit_pool_pass(p, C, pi, hTh, hTo, vmask, rcnt, a, b, coef)
        p.end_phase()
    return nc


def pool_inputs(core, npass, hT_full, pl_w, pl_b, pl_scale, vecs, ntok_total):
    tcore = npass * PTQ
    t0 = core * tcore
    hTh = np.zeros((D, tcore + 16), np.float32)
    lo, hi = max(0, t0 - 8), min(ntok_total, t0 + tcore + 8)
    hTh[:, lo - (t0 - 8):hi - (t0 - 8)] = hT_full[:, lo:hi]
    vm = np.zeros((npass, 128, PHW), np.float32)
    rc = np.zeros((npass, 128, 4 * PTQ), np.float32)
    for pi in range(npass):
        tt = t0 + pi * PTQ - 8 + np.arange(PHW)
        vm[pi] = ((tt >= 0) & (tt < ntok_total)).astype(np.float32)[None, :]
        t = t0 + pi * PTQ + np.arange(PTQ)
        for g, size in enumerate((2, 4, 8, 16)):
            lo_ = np.clip(t - size // 2, 0, ntok_total)
            hi_ = np.clip(t - size // 2 + size, 0, ntok_total)
            rc[pi, :, g * PTQ:(g + 1) * PTQ] = (1.0 / (hi_ - lo_).astype(np.float32))[None, :]
    m = {"hTh": hTh, "plw": np.ascontiguousarray(pl_w), "vmask": vm, "rcnt": rc,
         "sbv": np.concatenate([fm(pl_scale), fm(pl_b)], 1)}
    for k, v in vecs.items():
        m["v_" + k] = fm(v)
    return m


D3 = 3 * D
HC3 = D3 // 128


def fm3(v):
    return np.ascontiguousarray(np.asarray(v, np.float32).reshape(HC3, 128).T)


class HyInCtx(BaseCtx):
    def __init__(self, nc, p, st):
        super().__init__(nc, p, st)
        A = self.A
        self.Y = A("hiY", [128, DC * 514], F32)
        self.yR = [p.res() for _ in range(DC)]
        self.U = A("hiU", [128, DC * 514], BF16)
        self.uR = [p.res() for _ in range(DC)]
        self.W = [A("hiW%d" % i, [128, 16 * 128], BF16) for i in range(4)]
        self.WR = [p.res() for _ in range(4)]
        self.Z = [A("hiZ%d" % i, [128, 514], F32) for i in range(3)]
        self.zR = [p.res() for _ in range(3)]
        self.ZC = [A("hiZC%d" % i, [128, 512], F32) for i in range(6)]
        self.zcR = [p.res() for _ in range(6)]
        self.vm = A("hivm", [128, 514], F32)
        self.vmR = p.res()
        self.cv = A("hicv", [128, 5 * HC3], F32)
        self.cvR = p.res()
        self.stage = lambda dc, TP: (self.Y[:, dc * 514: dc * 514 + TP], [self.yR[dc]])
        self.zcnt = 0


def emit_hyin_pass(p, C, c0, TP, pi, hTh, x0T, vxT, o0, w_in, vmask, a, b):
    TH = TP + 2
    groups = [(o, min(512, TH - o)) for o in range(0, TH, 512)]
    p.op("sync", lambda e: [e.dma_start(out=C.vm[:, 0:TH], in_=vmask[pi, :, 0:TH])], writes=[C.vmR], dma=(C.vmR, 1))
    uv = lambda dc: C.U[:, dc * 514: dc * 514 + TH]
    emit_prenorm_mod(p, C, hTh, c0, TH, groups, a, b, uv, C.uR)
    cvx = lambda which, ch: C.cv[:, which * HC3 + ch: which * HC3 + ch + 1]
    for m in range(DC):
        zcs = []
        for part in range(3):
            ch = part * DC + m
            slot = C.cnt % 4
            C.cnt += 1
            W = C.W[slot]
            p.op("gpsimd", lambda e, W=W, ch=ch: [e.dma_start(
                out=W[:].rearrange("p (k c) -> p k c", c=128),
                in_=w_in[:, ch * 128:(ch + 1) * 128].rearrange("(k p) c -> p k c", p=128))],
                writes=[C.WR[slot]], dma=(C.WR[slot], 1))
            zs = C.zcnt % 3
            Z, ZR = C.Z[zs], C.zR[zs]
            banks = [((C.zcnt % 2) * 2 + gi) for gi in range(len(groups))]
            for gi, (off, n) in enumerate(groups):
                bkx = banks[gi]
                for k in range(DC):
                    p.op("tensor", lambda e, W=W, k=k, bkx=bkx, off=off, n=n: e.matmul(
                        C.ps[bkx][:, 0:n], W[:, k * 128:(k + 1) * 128], C.U[:, k * 514 + off: k * 514 + off + n],
                        start=(k == 0), stop=(k == DC - 1)), reads=[C.WR[slot], C.uR[k]], writes=[C.psR[bkx]])
                p.op("vector", lambda e, Z=Z, bkx=bkx, off=off, n=n, ch=ch: e.scalar_tensor_tensor(
                    out=Z[:, off:off + n], in0=C.ps[bkx][:, 0:n], scalar=cvx(0, ch), in1=C.vm[:, off:off + n],
                    op0=ALU.add, op1=ALU.mult), reads=[C.psR[bkx], C.cvR, C.vmR], writes=[ZR])
            zi = (m % 2) * 3 + part
            C.zcnt += 1
            ZC, ZCR = C.ZC[zi], C.zcR[zi]
            p.op("vector", lambda e, ZC=ZC, Z=Z, ch=ch: e.tensor_scalar(
                out=ZC[:, 0:TP], in0=Z[:, 1:1 + TP], scalar1=cvx(2, ch), scalar2=cvx(4, ch), op0=ALU.mult, op1=ALU.add),
                reads=[ZR, C.cvR], writes=[ZCR])
            p.op("vector", lambda e, ZC=ZC, Z=Z, ch=ch: e.scalar_tensor_tensor(
                out=ZC[:, 0:TP], in0=Z[:, 0:TP], scalar=cvx(1, ch), in1=ZC[:, 0:TP], op0=ALU.mult, op1=ALU.add),
                reads=[ZR, C.cvR, ZCR], writes=[ZCR])
            p.op("vector", lambda e, ZC=ZC, Z=Z, ch=ch: e.scalar_tensor_tensor(
                out=ZC[:, 0:TP], in0=Z[:, 2:2 + TP], scalar=cvx(3, ch), in1=ZC[:, 0:TP], op0=ALU.mult, op1=ALU.add),
                reads=[ZR, C.cvR, ZCR], writes=[ZCR])
            zcs.append((ZC, ZCR))
        (X0, X0R), (X1, X1R), (VV, VR) = zcs
        p.op("sync", lambda e, X0=X0, m=m: [e.dma_start(out=x0T[m * 128:(m + 1) * 128, o0:o0 + TP], in_=X0[:, 0:TP])],
             reads=[X0R], dma=(X0R, 1))
        p.op("gpsimd", lambda e, X1=X1, VV=VV: e.tensor_tensor(out=VV[:, 0:TP], in0=VV[:, 0:TP], in1=X1[:, 0:TP], op=ALU.mult),
             reads=[X1R, VR], writes=[VR])
        p.op("sync", lambda e, VV=VV, m=m: [e.dma_start(out=vxT[m * 128:(m + 1) * 128, o0:o0 + TP], in_=VV[:, 0:TP])],
             reads=[VR], dma=(VR, 1))


def build_hyin_launch(npass, TP, ctx_tp=0):
    nc = bass.Bass("TRN2", target_bir_lowering=False)
    dt = lambda n, s, k: nc.dram_tensor(n, s, F32, kind=k).ap()
    tcore = npass * TP
    hTh = dt("hTh", [D, tcore + 2], "ExternalInput")
    x0T = dt("x0T", [D, tcore], "ExternalOutput")
    vxT = dt("vxT", [D, tcore], "ExternalOutput")
    w_in = dt("w_in", [D, D3], "ExternalInput")
    vmask = dt("vmask", [npass, 128, 514], "ExternalInput")
    cv = dt("cv", [128, 5 * HC3], "ExternalInput")
    names = ["pre", "shift", "scale"]
    if ctx_tp:
        names += ["shift_c", "scale_c"]
        c_hTh = dt("c_hTh", [D, ctx_tp + 2], "ExternalInput")
        c_x0T = dt("c_x0T", [D, ctx_tp], "ExternalOutput")
        c_vxT = dt("c_vxT", [D, ctx_tp], "ExternalOutput")
        c_vmask = dt("c_vmask", [1, 128, 514], "ExternalInput")
    vec = {n: dt("v_" + n, [128, DC], "ExternalInput") for n in names}
    with ExitStack() as st:
        p = Prog(nc, st)
        C = HyInCtx(nc, p, st)
        V = load_vecs(nc, p, st, vec, names)
        V["post"] = V["pre"]
        V["gate"] = V["pre"]
        a, b, _ = make_mod_coefs(nc, p, st, V, "pre", "shift", "scale", "post", "gate", 1.0, "m")
        p.op("sync", lambda e: [e.dma_start(out=C.cv[:], in_=cv)], writes=[C.cvR], dma=(C.cvR, 1))
        for pi in range(npass):
            emit_hyin_pass(p, C, pi * TP, TP, pi, hTh, x0T, vxT, pi * TP, w_in, vmask, a, b)
        if ctx_tp:
            a2, b2, _ = make_mod_coefs(nc, p, st, V, "pre", "shift_c", "scale_c", "post", "gate", 1.0, "c")
            emit_hyin_pass(p, C, 0, ctx_tp, 0, c_hTh, c_x0T, c_vxT, 0, w_in, c_vmask, a2, b2)
        p.end_phase()
    return nc


def hyin_inputs(core, npass, TP, hT_full, w_in, b_in, w_sc, b_sc, vecs, ntok_total):
    tcore = npass * TP
    t0 = core * tcore
    hTh = np.zeros((D, tcore + 2), np.float32)
    lo, hi = max(0, t0 - 1), min(ntok_total, t0 + tcore + 1)
    hTh[:, lo - (t0 - 1):hi - (t0 - 1)] = hT_full[:, lo:hi]
    vm = np.zeros((npass, 128, 514), np.float32)
    for pi in range(npass):
        tt = t0 + pi * TP - 1 + np.arange(TP + 2)
        vm[pi, :, :TP + 2] = ((tt >= 0) & (tt < ntok_total)).astype(np.float32)[None, :]
    cv = np.concatenate([fm3(b_in), fm3(w_sc[0]), fm3(w_sc[1]), fm3(w_sc[2]), fm3(b_sc)], 1)
    m = {"hTh": hTh, "w_in": w_in, "vmask": vm, "cv": cv}
    for k in ("pre", "shift", "scale"):
        m["v_" + k] = fm(vecs[k])
    return m


class HyOutCtx(BaseCtx):
    def __init__(self, nc, p, st):
        super().__init__(nc, p, st)
        A = self.A
        self.Y = A("hoY", [128, DC * 512], F32)
        self.yR = [p.res() for _ in range(DC)]
        self.G = A("hoG", [128, DC * 512], BF16)
        self.gR = [p.res() for _ in range(DC)]
        self.W = [A("hoW%d" % i, [128, 16 * 128], BF16) for i in range(3)]
        self.WR = [p.res() for _ in range(3)]
        self.I = [A("hoI%d" % i, [128, 512], F32) for i in range(4)]
        self.iR = [p.res() for _ in range(4)]
        self.bo = A("hobo", [128, DC], F32)
        self.boR = p.res()


def emit_hyout_pass(p, C, t0, TP, hT, hTo, x0T, ycT, w_out, coef):
    for dc in range(DC):
        s = dc % 2
        I0, I1 = C.I[2 * s], C.I[2 * s + 1]
        p.op("sync", lambda e, I0=I0, dc=dc: [e.dma_start(out=I0[:, 0:TP], in_=x0T[dc * 128:(dc + 1) * 128, t0:t0 + TP])],
             writes=[C.iR[2 * s]], dma=(C.iR[2 * s], 1))
        p.op("sync", lambda e, I1=I1, dc=dc: [e.dma_start(out=I1[:, 0:TP], in_=ycT[dc * 128:(dc + 1) * 128, t0:t0 + TP])],
             writes=[C.iR[2 * s + 1]], dma=(C.iR[2 * s + 1], 1))
        p.op("vector", lambda e, I0=I0, I1=I1, dc=dc: e.tensor_tensor(
            out=C.G[:, dc * 512: dc * 512 + TP], in0=I0[:, 0:TP], in1=I1[:, 0:TP], op=ALU.mult),
            reads=[C.iR[2 * s], C.iR[2 * s + 1]], writes=[C.gR[dc]])
    for dc in range(DC):
        slot = C.cnt % 3
        C.cnt += 1
        W = C.W[slot]
        p.op("gpsimd", lambda e, W=W, dc=dc: [e.dma_start(
            out=W[:].rearrange("p (k c) -> p k c", c=128),
            in_=w_out[:, dc * 128:(dc + 1) * 128].rearrange("(k p) c -> p k c", p=128))],
            writes=[C.WR[slot]], dma=(C.WR[slot], 1))
        bkx = dc % 2
        for k in range(DC):
            p.op("tensor", lambda e, W=W, k=k, bkx=bkx: e.matmul(
                C.ps[bkx][:, 0:TP], W[:, k * 128:(k + 1) * 128], C.G[:, k * 512: k * 512 + TP],
                start=(k == 0), stop=(k == DC - 1)), reads=[C.WR[slot], C.gR[k]], writes=[C.psR[bkx]])
        s = dc % 2
        p.op("scalar", lambda e, dc=dc, bkx=bkx: e.activation(
            out=C.Y[:, dc * 512: dc * 512 + TP], in_=C.ps[bkx][:, 0:TP], func=AF.Identity, bias=C.bo[:, dc:dc + 1], scale=1.0),
            reads=[C.psR[bkx], C.boR], writes=[C.yR[dc]])
        p.op("scalar", lambda e, dc=dc, bkx=bkx, s=s: e.activation(
            out=C.sq[s][:, 0:TP], in_=C.ps[bkx][:, 0:TP], func=AF.Square, bias=C.bo[:, dc:dc + 1], scale=1.0),
            reads=[C.psR[bkx], C.boR], writes=[C.sqR[s]])
        p.op("tensor", lambda e, dc=dc, s=s: e.matmul(
            C.ps[7][:, 0:TP], C.ones[:], C.sq[s][:, 0:TP], start=(dc == 0), stop=(dc == DC - 1)),
            reads=[C.onesR, C.sqR[s]], writes=[C.psR[7]])
    yview = lambda dc: C.Y[:, dc * 512: dc * 512 + TP]
    emit_postnorm_residual(p, C, hT, hTo, t0, TP, [(0, TP)], coef, yview, C.yR, [7])


def build_hyout_launch(npass, TP, ctx_tp=0):
    nc = bass.Bass("TRN2", target_bir_lowering=False)
    dt = lambda n, s, k: nc.dram_tensor(n, s, F32, kind=k).ap()
    tcore = npass * TP
    hT = dt("hT", [D, tcore], "ExternalInput")
    x0T = dt("x0T", [D, tcore], "ExternalInput")
    ycT = dt("ycT", [D, tcore], "ExternalInput")
    hTo = dt("hTo", [D, tcore], "ExternalOutput")
    w_out = dt("w_out", [D, D], "ExternalInput")
    bo = dt("bo", [128, DC], "ExternalInput")
    names = ["pre", "post", "shift", "scale", "gate"]
    if ctx_tp:
        names += ["gate_c"]
        c_hT = dt("c_hT", [D, ctx_tp], "ExternalInput")
        c_x0T = dt("c_x0T", [D, ctx_tp], "ExternalInput")
        c_ycT = dt("c_ycT", [D, ctx_tp], "ExternalInput")
        c_hTo = dt("c_hTo", [D, ctx_tp], "ExternalOutput")
    vec = {n: dt("v_" + n, [128, DC], "ExternalInput") for n in names}
    with ExitStack() as st:
        p = Prog(nc, st)
        C = HyOutCtx(nc, p, st)
        V = load_vecs(nc, p, st, vec, names)
        _, _, coef = make_mod_coefs(nc, p, st, V, "pre", "shift", "scale", "post", "gate", 1.0, "m")
        p.op("sync", lambda e: [e.dma_start(out=C.bo[:], in_=bo)], writes=[C.boR], dma=(C.boR, 1))
        for pi in range(npass):
            emit_hyout_pass(p, C, pi * TP, TP, hT, hTo, x0T, ycT, w_out, coef)
        if ctx_tp:
            _, _, coef2 = make_mod_coefs(nc, p, st, V, "pre", "shift", "scale", "post", "gate_c", 1.0, "c")
            emit_hyout_pass(p, C, 0, ctx_tp, c_hT, c_hTo, c_x0T, c_ycT, w_out, coef2)
        p.end_phase()
    return nc


FN = 2 * SEQ
CPC = D // NCORES


def fft_tables():
    n1 = np.arange(128)[:, None].astype(np.float64)
    k1 = np.arange(256)[None, :].astype(np.float64)
    a1 = 2 * np.pi * n1 * k1 / 256
    F1T = np.concatenate([np.cos(a1), -np.sin(a1)], 1)
    n2 = np.arange(128)[:, None].astype(np.float64)
    at = 2 * np.pi * n2 * k1 / FN
    TW = np.concatenate([np.cos(at), -np.sin(at)], 1)
    k2 = np.arange(128)[None, :].astype(np.float64)
    a2 = 2 * np.pi * n2 * k2 / 128
    C2, S2 = np.cos(a2), np.sin(a2)
    F2 = np.concatenate([C2, S2, -S2], 1)
    F2I = np.concatenate([C2, S2, -S2, C2], 1)
    k1p = np.arange(128)[:, None].astype(np.float64)
    TWc = np.zeros((128, 2, 2, 128))
    CT = np.zeros((128, 2, 128))
    ST = np.zeros((128, 2, 128))
    for kt in range(2):
        kk = k1p + 128 * kt
        ang = 2 * np.pi * kk * np.arange(128)[None, :] / FN
        TWc[:, kt, 0] = np.cos(ang)
        TWc[:, kt, 1] = np.sin(ang)
        an = 2 * np.pi * kk * np.arange(128)[None, :] / 256
        CT[:, kt] = np.cos(an) / FN
        ST[:, kt] = -np.sin(an) / FN
    f = lambda x: np.ascontiguousarray(x.reshape(128, -1).astype(np.float32))
    return {"F1T": f(F1T), "TW": f(TW), "F2": f(F2), "F2I": f(F2I), "TWc": f(TWc), "CT": f(CT), "ST": f(ST)}


class FFTCtx:
    def __init__(self, nc, p, st, dram, inverse=True):
        self.nc, self.p, self.st = nc, p, st
        A = lambda n, s, d: st.enter_context(nc.sbuf_tensor(n, s, d))
        self.A = A
        self.tabR = p.res()
        self.F1T = A("fF1T", [128, 512], BF16)
        self.TW = A("fTW", [128, 512], F32)
        self.F2 = A("fF2", [128, 384], BF16)
        bf = [(self.F1T, "F1T"), (self.F2, "F2")]
        fp = [(self.TW, "TW")]
        if inverse:
            self.F2I = A("fF2I", [128, 512], BF16)
            self.TWc = A("fTWc", [128, 512], F32)
            self.CT = A("fCT", [128, 256], BF16)
            self.ST = A("fST", [128, 256], BF16)
            bf += [(self.F2I, "F2I"), (self.CT, "CT"), (self.ST, "ST")]
            fp += [(self.TWc, "TWc")]
        p.op("gpsimd", lambda e: [e.dma_start(out=t[:], in_=dram[n]) for t, n in bf], writes=[self.tabR],
             dma=(self.tabR, len(bf)))
        tR2 = p.res()
        self.tabR2 = tR2
        p.op("sync", lambda e: [e.dma_start(out=t[:], in_=dram[n]) for t, n in fp], writes=[tR2], dma=(tR2, len(fp)))
        self.ps = [st.enter_context(nc.psum_tensor("ps%d" % i, [128, 512], F32)) for i in range(8)]
        self.psR = [p.res("ps%d" % i) for i in range(8)]
        self.M = [A("fM%d" % i, [128, 256], F32) for i in range(8)]
        self.mR = [p.res() for _ in range(8)]
        self.mc = 0


def emit_cmul(p, C, in_r, in_i, inR, tab_r, tab_i, tabR, out_r, out_i, outR, shape3=None):
    s = (C.mc % 2) * 4
    C.mc += 1
    Ms = C.M[s:s + 4]
    Rs = C.mR[s:s + 4]
    mv = [m[:] if shape3 is None else m[:].rearrange("p (a b) -> p a b", a=shape3) for m in Ms]
    for (mi, a_, b_) in ((0, in_r, tab_r), (1, in_i, tab_i), (2, in_r, tab_i), (3, in_i, tab_r)):
        p.op("vector", lambda e, mi=mi, a_=a_, b_=b_: e.tensor_tensor(out=mv[mi], in0=a_, in1=b_, op=ALU.mult),
             reads=list(inR) + list(tabR), writes=[Rs[mi]])
    p.op("gpsimd", lambda e: e.tensor_tensor(out=out_r, in0=mv[0], in1=mv[1], op=ALU.subtract),
         reads=[Rs[0], Rs[1]], writes=list(outR))
    p.op("gpsimd", lambda e: e.tensor_tensor(out=out_i, in0=mv[2], in1=mv[3], op=ALU.add),
         reads=[Rs[2], Rs[3]], writes=list(outR))


def emit_fft_fwd_batch(p, C, xs, Q, qR, xbanks):
    nb = len(xs)
    for c, (xap, xR) in enumerate(xs):
        b = c % 2
        p.op("tensor", lambda e, xap=xap, b=b: e.matmul(C.ps[b][:, :], xap, C.F1T[:], start=True, stop=True),
             reads=list(xR) + [C.tabR], writes=[C.psR[b]])
        emit_cmul(p, C, C.ps[b][:, 0:256], C.ps[b][:, 256:512], [C.psR[b]], C.TW[:, 0:256], C.TW[:, 256:512], [C.tabR2],
                  Q[c][:, 0:256], Q[c][:, 256:512], [qR[c]])
    C2, S2, S2n = C.F2[:, 0:128], C.F2[:, 128:256], C.F2[:, 256:384]
    for (w, src, dst, first) in ((C2, 0, 0, True), (C2, 256, 256, False), (S2, 256, 0, False), (S2n, 0, 256, False)):
        for c in range(nb):
            xb = xbanks[c]
            p.op("tensor", lambda e, w=w, src=src, dst=dst, first=first, c=c, xb=xb: e.matmul(
                C.ps[xb][:, dst:dst + 256], w, Q[c][:, src:src + 256], start=first, stop=(not first and dst == 256 and w is S2n),
                skip_group_check=True), reads=[C.tabR, qR[c]], writes=[C.psR[xb]])


def build_hyconv_launch(nch, with_ctx=False):
    nc = bass.Bass("TRN2", target_bir_lowering=False)
    dt = lambda n, s, k, d=F32: nc.dram_tensor(n, s, d, kind=k).ap()
    vx = dt("vx", [nch, SEQ], "ExternalInput")
    G = dt("G", [nch, 128, 512], "ExternalInput", BF16)
    yc = dt("yc", [nch, SEQ], "ExternalOutput")
    tabs = {n: dt("t_" + n, list(v.shape), "ExternalInput") for n, v in fft_tables().items()}
    if with_ctx:
        vxc = dt("vxc", [256, CTX], "ExternalInput")
        gc = dt("gc", [256, 2 * CTX], "ExternalInput")
        skc = dt("skc", [128, 2], "ExternalInput")
        ycc = dt("ycc", [256, CTX], "ExternalOutput")
    CB = 4
    with ExitStack() as st:
        p = Prog(nc, st)
        C = FFTCtx(nc, p, st, tabs)
        A = C.A
        X = [A("cX%d" % i, [128, CB * 128], BF16) for i in range(2)]
        XR = [p.res() for _ in range(2)]
        Gt = [A("cG%d" % i, [128, CB * 512], BF16) for i in range(2)]
        GR = [p.res() for _ in range(2)]
        Q = [A("cQ%d" % i, [128, 512], BF16) for i in range(CB)]
        qR = [p.res() for _ in range(CB)]
        Yb = [A("cY%d" % i, [128, 512], BF16) for i in range(2)]
        yR = [p.res() for _ in range(2)]
        Sb = [A("cS%d" % i, [128, 4 * 512], BF16) for i in range(2)]
        sR = [p.res() for _ in range(2)]
        O = [A("cO%d" % i, [128, CB * 128], F32) for i in range(2)]
        oR = [p.res() for _ in range(2)]
        if with_ctx:
            xc = [A("cxc%d" % i, [128, CTX], F32) for i in range(2)]
            yy = [A("cyc%d" % i, [128, CTX], F32) for i in range(2)]
            gg = [A("cgc%d" % i, [128, 2 * CTX], F32) for i in range(2)]
            sk = A("csk", [128, 2], F32)
            cR = [p.res() for _ in range(2)]
            p.op("sync", lambda e: [e.dma_start(out=sk[:], in_=skc)] +
                 [e.dma_start(out=xc[i][:], in_=vxc[i * 128:(i + 1) * 128, :]) for i in range(2)] +
                 [e.dma_start(out=gg[i][:], in_=gc[i * 128:(i + 1) * 128, :]) for i in range(2)],
                 writes=cR, dma=(cR[0], 5))
            for i, eng in enumerate(("vector", "vector")):
                x_, y_, g_ = xc[i], yy[i], gg[i]
                p.op(eng, lambda e, x_=x_, y_=y_, i=i: e.tensor_scalar(out=y_[:], in0=x_[:], scalar1=sk[:, i:i + 1], scalar2=None,
                                                                    op0=ALU.mult), reads=[cR[0], cR[1]], writes=[cR[i]])
                for l in range(CTX):
                    p.op(eng, lambda e, x_=x_, y_=y_, g_=g_, l=l: e.scalar_tensor_tensor(
                        out=y_[:, l:CTX], in0=x_[:, 0:CTX - l], scalar=g_[:, l:l + 1], in1=y_[:, l:CTX],
                        op0=ALU.mult, op1=ALU.add), reads=[cR[i]], writes=[cR[i]])
                for l in range(1, CTX):
                    p.op(eng, lambda e, x_=x_, y_=y_, g_=g_, l=l: e.scalar_tensor_tensor(
                        out=y_[:, 0:CTX - l], in0=x_[:, l:CTX], scalar=g_[:, CTX + l:CTX + l + 1], in1=y_[:, 0:CTX - l],
                        op0=ALU.mult, op1=ALU.add), reads=[cR[i]], writes=[cR[i]])
                p.op("sync", lambda e, y_=y_, i=i: [e.dma_start(out=ycc[i * 128:(i + 1) * 128, :], in_=y_[:])],
                     reads=[cR[i]], dma=(cR[i], 1))
        for bi in range(nch // CB):
            s = bi % 2
            ch0 = bi * CB
            p.op("gpsimd", lambda e, s=s, ch0=ch0: [e.dma_start(
                out=X[s][:].rearrange("p (c n) -> p c n", c=CB),
                in_=vx[ch0:ch0 + CB, :].rearrange("c (a n) -> a c n", n=128))], writes=[XR[s]], dma=(XR[s], 1))
            p.op("sync", lambda e, s=s, ch0=ch0: [e.dma_start(
                out=Gt[s][:].rearrange("p (c n) -> p c n", c=CB),
                in_=G[ch0:ch0 + CB, :, :].rearrange("c a n -> a c n"))], writes=[GR[s]], dma=(GR[s], 1))
            xs = [(X[s][:, c * 128:(c + 1) * 128], [XR[s]]) for c in range(CB)]
            xbanks = [2, 3, 4, 5]
            emit_fft_fwd_batch(p, C, xs, Q, qR, xbanks)
            S4 = Sb[s][:].rearrange("p (kt ri c n) -> p kt ri c n", kt=2, ri=2, c=CB)
            for c in range(CB):
                xb = xbanks[c]
                ys = c % 2
                g0 = c * 512
                emit_cmul(p, C, C.ps[xb][:, 0:256], C.ps[xb][:, 256:512], [C.psR[xb]],
                          Gt[s][:, g0:g0 + 256], Gt[s][:, g0 + 256:g0 + 512], [GR[s]],
                          Yb[ys][:, 0:256], Yb[ys][:, 256:512], [yR[ys]])
                rb = 6 + c % 2
                for kt in range(2):
                    p.op("tensor", lambda e, kt=kt, ys=ys, rb=rb: e.matmul(
                        C.ps[rb][:, kt * 256:(kt + 1) * 256], Yb[ys][:, kt * 128:(kt + 1) * 128], C.F2I[:, 0:256],
                        start=(kt == 0), stop=False, skip_group_check=True), reads=[yR[ys], C.tabR], writes=[C.psR[rb]])
                    p.op("tensor", lambda e, kt=kt, ys=ys, rb=rb: e.matmul(
                        C.ps[rb][:, kt * 256:(kt + 1) * 256], Yb[ys][:, 256 + kt * 128:256 + (kt + 1) * 128], C.F2I[:, 256:512],
                        start=False, stop=(kt == 1), skip_group_check=True), reads=[yR[ys], C.tabR], writes=[C.psR[rb]])
                psv = C.ps[rb][:].rearrange("p (kt ri n) -> p kt ri n", kt=2, ri=2)
                twv = C.TWc[:].rearrange("p (kt ri n) -> p kt ri n", kt=2, ri=2)
                emit_cmul(p, C, psv[:, :, 0, :], psv[:, :, 1, :], [C.psR[rb]], twv[:, :, 0, :], twv[:, :, 1, :], [C.tabR2],
                          S4[:, :, 0, c, :], S4[:, :, 1, c, :], [sR[s]], shape3=2)
            ob = bi % 2
            i = 0
            for kt in range(2):
                for ri in range(2):
                    tab = C.CT if ri == 0 else C.ST
                    p.op("tensor", lambda e, kt=kt, ri=ri, tab=tab, s=s, ob=ob, i=i: e.matmul(
                        C.ps[ob][:, :], tab[:, kt * 128:(kt + 1) * 128], Sb[s][:, (kt * 2 + ri) * 512:(kt * 2 + ri + 1) * 512],
                        start=(i == 0), stop=(i == 3)), reads=[C.tabR, sR[s]], writes=[C.psR[ob]])
                    i += 1
            p.op("scalar", lambda e, s=s, ob=ob: e.activation(out=O[s][:], in_=C.ps[ob][:, :], func=AF.Copy),
                 reads=[C.psR[ob]], writes=[oR[s]])
            p.op("sync", lambda e, s=s, ch0=ch0: [e.dma_start(
                out=yc[ch0:ch0 + CB, :].rearrange("c (a n) -> a c n", n=128),
                in_=O[s][:].rearrange("p (c n) -> p c n", c=CB))], reads=[oR[s]], dma=(oR[s], 1))
        p.end_phase()
    return nc


TWO_PI = 2.0 * math.pi
SIN_OFF = math.pi + 16.0 * math.pi


def filt_consts(core):
    deltas = np.abs(np.linspace(math.log(1e-2) / 1.5, math.log(1e-2) / 0.3, D))
    ch = core * CPC + np.arange(CPC)
    dc_ = deltas[ch]
    order = np.concatenate([np.concatenate([hb * 128 + np.arange(128), hb * 128 + np.arange(128)]) for hb in range(2)])
    dl = (dc_[order] / (SEQ - 1))[None, :].repeat(64, 0)
    n1 = np.arange(128)[:, None]
    E1 = np.exp(-128.0 * n1 * dc_[order][None, :] / (SEQ - 1))
    L = SEQ
    t = np.linspace(0.0, 1.0, L)[:, None]
    om = 2.0 * math.pi * np.arange(L)[:, None] / L
    bands = np.linspace(1e-4, 15, 16)[None, :]
    z = np.concatenate([t, np.cos(bands * om), -np.sin(bands * om)], -1)
    Lc = CTX
    tcx = np.linspace(0.0, 1.0, Lc)[:, None]
    omc = 2.0 * math.pi * np.arange(Lc)[:, None] / Lc
    zc = np.concatenate([tcx, np.cos(bands * omc), -np.sin(bands * omc)], -1)
    tlin = np.linspace(0.0, 1.0, Lc)[None, :].repeat(128, 0)
    negd = -dc_.reshape(2, 128).T
    f = lambda x: np.ascontiguousarray(np.asarray(x, np.float32))
    return {"dl": f(dl), "E1": f(E1), "zt": f(z.T), "zc": f(zc.T), "tlin": f(tlin), "negd": f(negd)}, order


def build_hyfilt_launch():
    nc = bass.Bass("TRN2", target_bir_lowering=False)
    dt = lambda n, s, k, d=F32: nc.dram_tensor(n, s, d, kind=k).ap()
    zt = dt("zt", [33, SEQ], "ExternalInput")
    zc = dt("zc", [33, CTX], "ExternalInput")
    w1 = dt("w1", [33, 64], "ExternalInput")
    w23 = dt("w23", [64, 128], "ExternalInput")
    w4c = dt("w4c", [64, 512], "ExternalInput")
    fb = dt("fb", [64, 6], "ExternalInput")
    dl = dt("dl", [64, 512], "ExternalInput")
    E1 = dt("E1", [128, 512], "ExternalInput")
    skr = dt("skr", [128, CPC], "ExternalInput")
    tlin = dt("tlin", [128, CTX], "ExternalInput")
    negd = dt("negd", [128, 2], "ExternalInput")
    tabs = {n: dt("t_" + n, list(v.shape), "ExternalInput") for n, v in fft_tables().items() if n in ("F1T", "TW", "F2")}
    Gout = dt("Gout", [CPC, 128, 512], "ExternalOutput", BF16)
    gctx = dt("gctx", [CPC, 2 * CTX], "ExternalOutput")
    CB = 4
    with ExitStack() as st:
        p = Prog(nc, st)
        C = FFTCtx(nc, p, st, tabs, inverse=False)
        A = C.A
        cR = p.res()
        W1, W23, W4, FB = A("gw1", [33, 64], F32), A("gw23", [64, 128], F32), A("gw4", [64, 512], F32), A("gfb", [64, 9], F32)
        DL, E1t, SK = A("gdl", [64, 512], F32), A("ge1", [128, 512], F32), A("gsk", [128, CPC], F32)
        TL, ND = A("gtl", [128, CTX], F32), A("gnd", [128, 2], F32)
        p.op("sync", lambda e: [e.dma_start(out=a_[:], in_=b_) for a_, b_ in (
            (W1, w1), (W23, w23), (W4, w4c), (DL, dl), (E1t, E1), (SK, skr), (TL, tlin), (ND, negd))] +
            [e.dma_start(out=FB[:, 0:6], in_=fb)], writes=[cR], dma=(cR, 9))
        p.op("vector", lambda e: e.tensor_tensor(out=FB[:, 6:9], in0=FB[:, 0:3], in1=FB[:, 3:6], op=ALU.mult),
             reads=[cR], writes=[cR])
        NP = A("gnp", [128, 1], F32)
        p.op("vector", lambda e: e.memset(NP[:], -math.pi), writes=[cR])
        F3 = A("gf3", [64, SEQ], F32)
        f3R = p.res()
        F3c = A("gf3c", [64, CTX], F32)
        f3cR = p.res()
        ZB = [A("gzb%d" % i, [33, 512], F32) for i in range(2)]
        zbR = [p.res() for _ in range(2)]
        FA = [A("gfa%d" % i, [64, 512], F32) for i in range(4)]
        faR = [p.res() for _ in range(4)]
        AR = [A("gar%d" % i, [64, 512], F32) for i in range(2)]
        arR = [p.res() for _ in range(2)]
        WT = [A("gwt%d" % i, [64, 512], F32) for i in range(2)]
        wtR = [p.res() for _ in range(2)]
        cnt = [0, 0]

        def mlp_block(zsrc, n, dst, dstR):
            zs = cnt[0] % 2
            cnt[0] += 1
            p.op("sync", lambda e: [e.dma_start(out=ZB[zs][:, 0:n], in_=zsrc)], writes=[zbR[zs]], dma=(zbR[zs], 1))
            cur, curR = ZB[zs], zbR[zs]
            for l in range(3):
                bk = cnt[1] % 2
                cnt[1] += 1
                lhs = W1[:, :] if l == 0 else W23[:, (l - 1) * 64:l * 64]
                K = 33 if l == 0 else 64
                p.op("tensor", lambda e, lhs=lhs, cur=cur, bk=bk, K=K: e.matmul(
                    C.ps[bk][0:64, 0:n], lhs, cur[0:K, 0:n], start=True, stop=True), reads=[cR, curR], writes=[C.psR[bk]])
                a_ = AR[bk]
                p.op("vector", lambda e, a_=a_, bk=bk, l=l: e.tensor_scalar(
                    out=a_[:, 0:n], in0=C.ps[bk][0:64, 0:n], scalar1=FB[:, 3 + l:4 + l], scalar2=FB[:, 6 + l:7 + l],
                    op0=ALU.mult, op1=ALU.add), reads=[C.psR[bk], cR], writes=[arR[bk]])
                for (cmp_, thr, add_) in ((ALU.is_lt, -math.pi, TWO_PI), (ALU.is_gt, math.pi, -TWO_PI)):
                    p.op("vector", lambda e, a_=a_, cmp_=cmp_, thr=thr, add_=add_, bk=bk: e.tensor_scalar(
                        out=WT[bk][:, 0:n], in0=a_[:, 0:n], scalar1=thr, scalar2=add_, op0=cmp_, op1=ALU.mult),
                        reads=[arR[bk]], writes=[wtR[bk]])
                    p.op("vector", lambda e, a_=a_, bk=bk: e.tensor_tensor(
                        out=a_[:, 0:n], in0=a_[:, 0:n], in1=WT[bk][:, 0:n], op=ALU.add),
                        reads=[arR[bk], wtR[bk]], writes=[arR[bk]])
                if l < 2:
                    fs = cnt[1] % 4
                    o_, oR_ = FA[fs][:, 0:n], faR[fs]
                    nxt = FA[fs]
                else:
                    o_, oR_ = dst, dstR
                    nxt = None
                p.op("scalar", lambda e, a_=a_, o_=o_: e.activation(out=o_, in_=a_[:, 0:n], func=AF.Sin),
                     reads=[arR[bk], cR], writes=[oR_])
                cur, curR = nxt, oR_

        for blk in range(SEQ // 512):
            mlp_block(zt[:, blk * 512:(blk + 1) * 512], 512, F3[:, blk * 512:(blk + 1) * 512], f3R)
        mlp_block(zc[:, :], CTX, F3c[:, :], f3cR)
        GC = [A("ggc%d" % i, [128, CTX], F32) for i in range(2)]
        gcR = [p.res() for _ in range(2)]
        DCt = [A("gdc%d" % i, [128, CTX], F32) for i in range(2)]
        dcR = [p.res() for _ in range(2)]
        for j in range(2):
            p.op("scalar", lambda e, j=j: e.activation(out=DCt[j][:], in_=TL[:], func=AF.Exp, scale=ND[:, j:j + 1]),
                 reads=[cR], writes=[dcR[j]])
            for fbw in range(2):
                bk = 2 + (j * 2 + fbw) % 2
                col0 = j * 256 + fbw * 128
                p.op("tensor", lambda e, bk=bk, col0=col0: e.matmul(
                    C.ps[bk][:, 0:CTX], W4[:, col0:col0 + 128], F3c[:, :], start=True, stop=True),
                    reads=[cR, f3cR], writes=[C.psR[bk]])
                s = (j * 2 + fbw) % 2
                p.op("vector", lambda e, bk=bk, j=j, s=s: e.tensor_tensor(
                    out=GC[s][:], in0=C.ps[bk][:, 0:CTX], in1=DCt[j][:], op=ALU.mult),
                    reads=[C.psR[bk], dcR[j]], writes=[gcR[s]])
                p.op("sync", lambda e, s=s, j=j, fbw=fbw: [e.dma_start(
                    out=gctx[j * 128:(j + 1) * 128, fbw * CTX:(fbw + 1) * CTX], in_=GC[s][:])], reads=[gcR[s]], dma=(gcR[s], 1))
        Gm = A("gGm", [128, 256 * 128], BF16)
        gmR = p.res()
        Gm3 = Gm[:].rearrange("p (c n) -> p c n", n=128)
        E2 = [A("ge2%d" % i, [64, 256], F32) for i in range(2)]
        e2R = [p.res() for _ in range(2)]
        RH = [A("grh%d" % i, [64, 256], F32) for i in range(2)]
        rhR = [p.res() for _ in range(2)]
        Q = [A("gQ%d" % i, [128, 512], BF16) for i in range(CB)]
        qR = [p.res() for _ in range(CB)]
        FW = [A("gFW%d" % i, [128, 512], F32) for i in range(CB)]
        fwR = [p.res() for _ in range(CB)]
        GO = [A("gGO%d" % i, [128, CB * 512], BF16) for i in range(2)]
        goR = [p.res() for _ in range(2)]
        F3s = F3[:].rearrange("p (a n) -> p n a", n=128)
        for hb in range(2):
            c0 = hb * 256
            for n2 in range(128):
                s = n2 % 2
                p.op("scalar", lambda e, s=s, n2=n2, c0=c0: e.activation(out=E2[s][:], in_=DL[:, c0:c0 + 256], func=AF.Exp, scale=-float(n2)),
                     reads=[cR], writes=[e2R[s]])
                p.op("vector", lambda e, s=s, c0=c0: e.tensor_tensor(out=RH[s][:], in0=W4[:, c0:c0 + 256], in1=E2[s][:], op=ALU.mult),
                     reads=[cR, e2R[s]], writes=[rhR[s]])
                bk = n2 % 2
                p.op("tensor", lambda e, s=s, bk=bk, n2=n2: e.matmul(
                    C.ps[bk][:, 0:256], F3s[:, n2, :], RH[s][:], start=True, stop=True),
                    reads=[f3R, rhR[s]], writes=[C.psR[bk]])
                p.op("vector", lambda e, bk=bk, n2=n2, c0=c0: e.tensor_tensor(
                    out=Gm3[:, :, n2], in0=C.ps[bk][:, 0:256], in1=E1t[:, c0:c0 + 256], op=ALU.mult),
                    reads=[C.psR[bk], cR], writes=[gmR])
            p.op("vector", lambda e: e.memset(Gm3[0:1, 128:256, 0:1], 0.0), reads=[gmR], writes=[gmR])
            for bi in range(128 // CB):
                xbanks = [2, 3, 4, 5]
                chl = [bi * CB + c for c in range(CB)]
                xs = [(Gm3[:, ch, :], [gmR]) for ch in chl]
                emit_fft_fwd_batch(p, C, xs, Q, qR, xbanks)
                for c in range(CB):
                    xb = xbanks[c]
                    gch = hb * 128 + chl[c]
                    p.op("scalar", lambda e, c=c, xb=xb, gch=gch: e.activation(
                        out=FW[c][:, 0:256], in_=C.ps[xb][:, 0:256], func=AF.Identity, bias=SK[:, gch:gch + 1], scale=1.0),
                        reads=[C.psR[xb], cR], writes=[fwR[c]])
                    p.op("scalar", lambda e, c=c, xb=xb: e.activation(
                        out=FW[c][:, 256:512], in_=C.ps[xb][:, 256:512], func=AF.Copy),
                        reads=[C.psR[xb]], writes=[fwR[c]])
                xs = [(Gm3[:, 128 + ch, :], [gmR]) for ch in chl]
                emit_fft_fwd_batch(p, C, xs, Q, qR, xbanks)
                gs = bi % 2
                for c in range(CB):
                    xb = xbanks[c]
                    p.op("vector", lambda e, c=c, xb=xb, gs=gs: e.tensor_tensor(
                        out=GO[gs][:, c * 512:c * 512 + 256], in0=C.ps[xb][:, 0:256], in1=FW[c][:, 0:256], op=ALU.add),
                        reads=[C.psR[xb], fwR[c]], writes=[goR[gs]])
                    p.op("vector", lambda e, c=c, xb=xb, gs=gs: e.tensor_tensor(
                        out=GO[gs][:, c * 512 + 256:(c + 1) * 512], in0=FW[c][:, 256:512], in1=C.ps[xb][:, 256:512], op=ALU.subtract),
                        reads=[C.psR[xb], fwR[c]], writes=[goR[gs]])
                g0 = hb * 128 + bi * CB
                p.op("sync", lambda e, gs=gs, g0=g0: [e.dma_start(
                    out=Gout[g0:g0 + CB, :, :].rearrange("c a n -> a c n"),
                    in_=GO[gs][:].rearrange("p (c n) -> p c n", c=CB))], reads=[goR[gs]], dma=(goR[gs], 1))
        p.end_phase()
    return nc


def hyfilt_inputs(core, f_w1, f_b1, f_w2, f_b2, f_w3, f_b3, f_w4, f_freq, skip):
    cst, order = filt_consts(core)
    ch = core * CPC + np.arange(CPC)
    cols = []
    for hb in range(2):
        cols += list(ch[hb * 128:(hb + 1) * 128]) + list(D + ch[hb * 128:(hb + 1) * 128])
    f = lambda x: np.ascontiguousarray(np.asarray(x, np.float32))
    m = {"zt": cst["zt"], "zc": cst["zc"], "w1": f(f_w1), "w23": f(np.concatenate([f_w2, f_w3], 1)),
         "w4c": f(np.asarray(f_w4)[:, cols]),
         "fb": f(np.stack([f_b1, f_b2, f_b3, f_freq[0], f_freq[1], f_freq[2]], 1)),
         "dl": cst["dl"], "E1": cst["E1"], "skr": f(np.asarray(skip, np.float32)[ch][None, :].repeat(128, 0)),
         "tlin": cst["tlin"], "negd": cst["negd"]}
    tb = fft_tables()
    for n in ("F1T", "TW", "F2"):
        m["t_" + n] = tb[n]
    return m


_NC_CACHE = {}


def _get_nc(key, builder):
    if key not in _NC_CACHE:
        _NC_CACHE[key] = builder()
    return _NC_CACHE[key]


def _run(nc, in_maps):
    return run_bass_kernel_spmd(nc, in_maps, core_ids=list(range(NCORES))).results


def _cat_cols(res, name):
    return np.ascontiguousarray(np.concatenate([r[name] for r in res], axis=1))


def run_ffn(hT, hcT, w_in, w_out, pre, post, m3, mc3):
    with_ctx = hcT is not None
    tcx = CTX // NCORES
    nc = _get_nc(("ffn", with_ctx), lambda: build_ffn_launch(TCORE, with_ctx, tcx))
    w_in = np.ascontiguousarray(w_in, np.float32)
    w_out = np.ascontiguousarray(w_out, np.float32)
    base = {"w_in": w_in, "w_out": w_out, "v_pre": fm(pre), "v_post": fm(post),
            "v_shift": fm(m3[0]), "v_scale": fm(m3[1]), "v_gate": fm(m3[2])}
    if with_ctx:
        base.update({"v_shift_c": fm(mc3[0]), "v_scale_c": fm(mc3[1]), "v_gate_c": fm(mc3[2])})
    in_maps = []
    for c in range(NCORES):
        m = dict(base)
        m["hT"] = np.ascontiguousarray(hT[:, c * TCORE:(c + 1) * TCORE])
        if with_ctx:
            m["cT"] = np.ascontiguousarray(hcT[:, c * tcx:(c + 1) * tcx])
        in_maps.append(m)
    res = _run(nc, in_maps)
    return _cat_cols(res, "hTo"), (_cat_cols(res, "cTo") if with_ctx else None)


def run_attn(hT, hcT, w_qkv, b_qkv, sink, w_o, b_o, pre, post, m3, mc3):
    npass = TCORE // ATQ
    nc = _get_nc(("attn",), lambda: build_attn_launch(npass))
    vecs = {"pre": pre, "post": post, "shift": m3[0], "scale": m3[1], "gate": m3[2], "shift_c": mc3[0], "scale_c": mc3[1]}
    w_qkv = np.ascontiguousarray(w_qkv, np.float32)
    w_o = np.ascontiguousarray(w_o, np.float32)
    in_maps = [attn_inputs(c, npass, hT, hcT, w_qkv, b_qkv, sink, w_o, b_o, vecs, SEQ) for c in range(NCORES)]
    return _cat_cols(_run(nc, in_maps), "hTo")


def run_pool(hT, pl_w, pl_b, pl_scale, pre, post, m3):
    npass = TCORE // PTQ
    nc = _get_nc(("pool",), lambda: build_pool_launch(npass))
    vecs = {"pre": pre, "post": post, "shift": m3[0], "scale": m3[1], "gate": m3[2]}
    in_maps = [pool_inputs(c, npass, hT, np.asarray(pl_w, np.float32), pl_b, pl_scale, vecs, SEQ) for c in range(NCORES)]
    return _cat_cols(_run(nc, in_maps), "hTo")


def run_hyena(hT, hcT, hp, pre, post, m3, mc3):
    with_ctx = hcT is not None
    tcx = CTX // NCORES
    f32 = lambda a: np.ascontiguousarray(a, np.float32)
    ncf = _get_nc(("hyfilt",), build_hyfilt_launch)
    resf = _run(ncf, [hyfilt_inputs(c, hp["f_w1"], hp["f_b1"], hp["f_w2"], hp["f_b2"], hp["f_w3"], hp["f_b3"],
                                    hp["f_w4"], hp["f_freq"], hp["skip"]) for c in range(NCORES)])
    npass, TP = TCORE // 512, 512
    nci = _get_nc(("hyin", with_ctx), lambda: build_hyin_launch(npass, TP, tcx if with_ctx else 0))
    vecs = {"pre": pre, "shift": m3[0], "scale": m3[1]}
    w_in = f32(hp["w_in"])
    in_maps = []
    for c in range(NCORES):
        m = hyin_inputs(c, npass, TP, hT, w_in, hp["b_in"], hp["w_sc"], hp["b_sc"], vecs, SEQ)
        if with_ctx:
            mc = hyin_inputs(c, 1, tcx, hcT, w_in, hp["b_in"], hp["w_sc"], hp["b_sc"], vecs, CTX)
            m["c_hTh"] = mc["hTh"]
            m["c_vmask"] = mc["vmask"]
            m["v_shift_c"] = fm(mc3[0])
            m["v_scale_c"] = fm(mc3[1])
        in_maps.append(m)
    resi = _run(nci, in_maps)
    x0T, vxT = _cat_cols(resi, "x0T"), _cat_cols(resi, "vxT")
    ncc = _get_nc(("hyconv", with_ctx), lambda: build_hyconv_launch(CPC, with_ctx))
    tabs = fft_tables()
    if with_ctx:
        c_x0T, c_vxT = _cat_cols(resi, "c_x0T"), _cat_cols(resi, "c_vxT")
    in_maps = []
    for c in range(NCORES):
        m = {"vx": np.ascontiguousarray(vxT[c * CPC:(c + 1) * CPC, :]), "G": resf[c]["Gout"]}
        for n, v in tabs.items():
            m["t_" + n] = v
        if with_ctx:
            m["vxc"] = np.ascontiguousarray(c_vxT[c * CPC:(c + 1) * CPC, :])
            m["gc"] = resf[c]["gctx"]
            m["skc"] = np.ascontiguousarray(np.asarray(hp["skip"], np.float32)[c * CPC:(c + 1) * CPC].reshape(2, 128).T)
        in_maps.append(m)
    resc = _run(ncc, in_maps)
    ycT = np.ascontiguousarray(np.concatenate([r["yc"] for r in resc], axis=0))
    nco = _get_nc(("hyout", with_ctx), lambda: build_hyout_launch(npass, TP, tcx if with_ctx else 0))
    if with_ctx:
        c_ycT = np.concatenate([r["ycc"] for r in resc], axis=0)
    base = {"w_out": f32(hp["w_out"]), "bo": fm(hp["b_out"]), "v_pre": fm(pre), "v_post": fm(post),
            "v_shift": fm(m3[0]), "v_scale": fm(m3[1]), "v_gate": fm(m3[2])}
    if with_ctx:
        base["v_gate_c"] = fm(mc3[2])
    in_maps = []
    for c in range(NCORES):
        sl = slice(c * TCORE, (c + 1) * TCORE)
        m = dict(base)
        m.update({"hT": np.ascontiguousarray(hT[:, sl]), "x0T": np.ascontiguousarray(x0T[:, sl]),
                  "ycT": np.ascontiguousarray(ycT[:, sl])})
        if with_ctx:
            cs = slice(c * tcx, (c + 1) * tcx)
            m.update({"c_hT": np.ascontiguousarray(hcT[:, cs]), "c_x0T": np.ascontiguousarray(c_x0T[:, cs]),
                      "c_ycT": np.ascontiguousarray(c_ycT[:, cs])})
        in_maps.append(m)
    reso = _run(nco, in_maps)
    return _cat_cols(reso, "hTo"), (_cat_cols(reso, "c_hTo") if with_ctx else None)


def kernel(x, c, ctx, c_ctx, w_ada, b_ada, norm_pre, norm_post, w_ffn_in, w_ffn_out,
           hy_w_in, hy_b_in, hy_w_sc, hy_b_sc, hy_f_w1, hy_f_b1, hy_f_w2, hy_f_b2,
           hy_f_w3, hy_f_b3, hy_f_w4, hy_f_freq, hy_skip, hy_w_out, hy_b_out,
           at_w_qkv, at_b_qkv, at_sink, at_w_o, at_b_o, pl_w, pl_b, pl_scale):
    A = lambda v: np.asarray(v)
    hT = np.ascontiguousarray(A(x)[0].T.astype(np.float32))
    hcT = np.ascontiguousarray(A(ctx)[0].T.astype(np.float32))
    mod = run_mod(A(c), A(c_ctx), A(w_ada), A(b_ada))
    last_ctx_layer = 1
    for i in range(DEPTH):
        kind, j = i % 3, i // 3
        ctx_live = i <= last_ctx_layer
        ctx_out = i < last_ctx_layer
        m, mc = mod[i, 0], mod[i, 1]
        npre, npost = A(norm_pre)[i], A(norm_post)[i]
        hT, hc_new = run_ffn(hT, hcT if ctx_live else None, A(w_ffn_in)[i, 0], A(w_ffn_out)[i, 0],
                             npre[0], npost[0], m[0:3], mc[0:3])
        if ctx_live:
            hcT = hc_new
        if kind == 0:
            hp = {"w_in": A(hy_w_in)[j], "b_in": A(hy_b_in)[j], "w_sc": A(hy_w_sc)[j], "b_sc": A(hy_b_sc)[j],
                  "f_w1": A(hy_f_w1)[j], "f_b1": A(hy_f_b1)[j], "f_w2": A(hy_f_w2)[j], "f_b2": A(hy_f_b2)[j],
                  "f_w3": A(hy_f_w3)[j], "f_b3": A(hy_f_b3)[j], "f_w4": A(hy_f_w4)[j], "f_freq": A(hy_f_freq)[j],
                  "skip": A(hy_skip)[j], "w_out": A(hy_w_out)[j], "b_out": A(hy_b_out)[j]}
            hT, hc_new = run_hyena(hT, hcT if ctx_out else None, hp, npre[1], npost[1], m[3:6], mc[3:6])
            if ctx_out:
                hcT = hc_new
        elif kind == 1:
            hT = run_attn(hT, hcT, A(at_w_qkv)[j], A(at_b_qkv)[j], A(at_sink)[j], A(at_w_o)[j], A(at_b_o)[j],
                          npre[1], npost[1], m[3:6], mc[3:6])
        else:
            hT = run_pool(hT, A(pl_w)[j], A(pl_b)[j], A(pl_scale)[j], npre[1], npost[1], m[3:6])
        hT, hc_new = run_ffn(hT, hcT if ctx_out else None, A(w_ffn_in)[i, 1], A(w_ffn_out)[i, 1],
                             npre[2], npost[2], m[6:9], mc[6:9])
        if ctx_out:
            hcT = hc_new
    return np.ascontiguousarray(hT.T)[None].astype(np.float32)
```
